# Optimizing a Trainium2 kernel written in Bass

```python
import jax, jax.numpy as jnp
from jax import lax
import numpy as np

D_MODEL = 1024
BATCH = 2
SEQ = 8192
DEPTH = 2

PLE_DIM = 256
D_CONF = D_MODEL // 2
D_SC = D_MODEL // 2
N_GROUPS_CONF = 8
N_GROUPS_SC = 8
CONF_KERNEL = 31
SC_KERNEL = 3
FFN_KERNEL = 3
D_FF = 2816
EPS = 1e-6

W_IN_COLS = 2 * D_CONF + 3 * D_SC + 2 * D_MODEL

kernel_name = "hybrid_conformer_shortconv_gated_merge"


def rmsnorm(x, g):
    xf = x.astype(jnp.float32)
    y = xf * lax.rsqrt(jnp.mean(xf * xf, axis=-1, keepdims=True) + EPS)
    return (y * g.astype(jnp.float32)).astype(x.dtype)


def layernorm(x, g, b):
    xf = x.astype(jnp.float32)
    mu = jnp.mean(xf, axis=-1, keepdims=True)
    var = jnp.mean(jnp.square(xf - mu), axis=-1, keepdims=True)
    y = (xf - mu) * lax.rsqrt(var + EPS)
    return (y * g.astype(jnp.float32) + b.astype(jnp.float32)).astype(x.dtype)


def causal_dwconv(u, w):
    k, c = w.shape
    return lax.conv_general_dilated(
        u, w[:, None, :].astype(u.dtype),
        window_strides=(1,), padding=[(k - 1, 0)],
        dimension_numbers=("NWC", "WIO", "NWC"),
        feature_group_count=c)


def setup_inputs(seed: int = 0) -> dict:
    key = jax.random.key(seed)
    ks = jax.random.split(key, 24)
    f32 = jnp.float32

    def dense(k, shape, fan_in):
        return jax.random.normal(k, shape, f32) * (fan_in ** -0.5)

    def gain(k, shape):
        return 1.0 + 0.05 * jax.random.normal(k, shape, f32)

    def small(k, shape):
        return 0.02 * jax.random.normal(k, shape, f32)

    L = DEPTH
    return {
        "x": jax.random.normal(ks[0], (BATCH, SEQ, D_MODEL), f32),
        "p": jax.random.normal(ks[1], (DEPTH, BATCH, SEQ, PLE_DIM), f32),
        "g_mix": gain(ks[2], (L, D_MODEL)),
        "w_in": dense(ks[3], (L, D_MODEL, W_IN_COLS), D_MODEL),
        "b_gate": small(ks[4], (L, 2 * D_MODEL)),
        "conv_a_w": dense(ks[5], (L, CONF_KERNEL, D_CONF), CONF_KERNEL),
        "conv_a_b": small(ks[6], (L, D_CONF)),
        "ln_a_g": gain(ks[7], (L, D_CONF)),
        "ln_a_b": small(ks[8], (L, D_CONF)),
        "w_a_out": dense(ks[9], (L, D_CONF, D_MODEL), D_CONF),
        "conv_b_w": dense(ks[10], (L, SC_KERNEL, D_SC), SC_KERNEL),
        "w_b_out": dense(ks[11], (L, D_SC, D_MODEL), D_SC),
        "w_o": dense(ks[12], (L, D_MODEL, D_MODEL), D_MODEL),
        "g_ffn": gain(ks[13], (L, D_MODEL)),
        "w_up": dense(ks[14], (L, D_MODEL, 2 * D_FF), D_MODEL),
        "conv_f_w": dense(ks[15], (L, FFN_KERNEL, D_FF), FFN_KERNEL),
        "conv_f_b": small(ks[16], (L, D_FF)),
        "w_down": dense(ks[17], (L, D_FF, D_MODEL), D_FF),
        "g_ple": gain(ks[18], (L, D_MODEL)),
        "w_ple": dense(ks[19], (L, PLE_DIM, D_MODEL), PLE_DIM),
        "w_ple_gate": dense(ks[20], (L, D_MODEL, D_MODEL), D_MODEL),
        "g_final": gain(ks[21], (D_MODEL,)),
    }


def reference(x, p, g_mix, w_in, b_gate, conv_a_w, conv_a_b, ln_a_g, ln_a_b, w_a_out,
              conv_b_w, w_b_out, w_o, g_ffn, w_up, conv_f_w, conv_f_b, w_down,
              g_ple, w_ple, w_ple_gate, g_final):
    o1 = D_CONF
    o2 = o1 + D_CONF
    o3 = o2 + D_SC
    o4 = o3 + D_SC
    o5 = o4 + D_SC
    o6 = o5 + D_MODEL
    for i in range(DEPTH):
        h = rmsnorm(x, g_mix[i])
        z = jnp.einsum("bsd,dn->bsn", h, w_in[i])
        a_val, a_gt = z[..., :o1], z[..., o1:o2]
        sc_b, sc_c, sc_v = z[..., o2:o3], z[..., o3:o4], z[..., o4:o5]
        gate_logits = z[..., o5:] + b_gate[i]
        g_a = jax.nn.sigmoid(gate_logits[..., :D_MODEL])
        g_b = jax.nn.sigmoid(gate_logits[..., D_MODEL:])

        a = a_val * jax.nn.sigmoid(a_gt)
        a = causal_dwconv(a, conv_a_w[i]) + conv_a_b[i]
        a = jax.nn.silu(layernorm(a, ln_a_g[i], ln_a_b[i]))
        y_a = jnp.einsum("bsc,cd->bsd", a, w_a_out[i])

        s = sc_b * causal_dwconv(sc_c * sc_v, conv_b_w[i])
        y_b = jnp.einsum("bsc,cd->bsd", s, w_b_out[i])

        x = x + jnp.einsum("bsd,de->bse", g_a * y_a + g_b * y_b, w_o[i])

        h = rmsnorm(x, g_ffn[i])
        u = jnp.einsum("bsd,df->bsf", h, w_up[i])
        f_gate = causal_dwconv(u[..., :D_FF], conv_f_w[i]) + conv_f_b[i]
        f = jax.nn.gelu(f_gate, approximate=True) * u[..., D_FF:]
        x = x + jnp.einsum("bsf,fd->bsd", f, w_down[i])

        pg = jax.nn.sigmoid(jnp.einsum("bsd,de->bse", rmsnorm(x, g_ple[i]), w_ple_gate[i]))
        x = x + pg * jnp.einsum("bsk,kd->bsd", p[i], w_ple[i])

    return rmsnorm(x, g_final)
```

```python
import numpy as np
import concourse.bass as bass
import concourse.mybir as mybir
from concourse.bass_utils import run_bass_kernel_spmd

F32 = mybir.dt.float32
BF16 = mybir.dt.bfloat16
AF = mybir.ActivationFunctionType
ALU = mybir.AluOpType

D = 1024
NCH = 8
HALO = 64
OWN = 1024
TT = HALO + OWN
TILES = [(0, 384), (384, 352), (736, 352)]
DEPTH = 2
DFF = 2816
NFF = 22
EPS = 1e-6
NCORES = 8

V_GMIX = 0
V_BGA = 8
V_BGB = 16
V_CAW = 24
V_CAB = 148
V_LNG = 152
V_LNB = 156
V_CBW = 160
V_GFFN = 172
V_CFW = 180
V_CFB = 246
V_GPLE = 268
V_GFIN = 276
NV = 284

RING_SLOTS = 10
STRICT_SAME_ENGINE = True
SLOT_ELEMS = 2048


class Buf:
    __slots__ = ("name", "w", "r", "coarse")

    def __init__(self, name, coarse=False):
        self.name = name
        self.w = None
        self.r = {}
        self.coarse = coarse


class Sched:
    ENGS = ("pe", "act", "dve", "pool", "sp")

    def __init__(self):
        self.ops = {e: [] for e in self.ENGS}
        self.count = {}
        self.clk = {e: {} for e in self.ENGS}
        self.done = {}

    def op(self, eng, fn, reads=(), writes=(), sem=None, inc=1):
        key = sem if sem is not None else "E_" + eng
        clk = self.clk[eng]
        cand = {}

        def need(k, v):
            if v > clk.get(k, 0) and v > cand.get(k, 0):
                cand[k] = v

        for b in reads:
            if b.w is not None:
                need(*b.w)
        strict = STRICT_SAME_ENGINE and eng in ("act", "dve")
        for b in writes:
            st_b = strict and not b.coarse
            if b.w is not None and (st_b or b.w[0] != key):
                need(*b.w)
            for k, v in b.r.items():
                if st_b or k != key:
                    need(k, v)
        waits = []
        for k in sorted(cand, key=lambda k: (k == key, k)):
            v = cand[k]
            if clk.get(k, 0) >= v:
                continue
            waits.append((k, v))
            for kk, vv in self.done[(k, v)].items():
                if vv > clk.get(kk, 0):
                    clk[kk] = vv
        val = self.count.get(key, 0) + inc
        self.count[key] = val
        d = dict(clk)
        d[key] = val
        self.done[(key, val)] = d
        for b in reads:
            b.r[key] = val
        for b in writes:
            b.w = (key, val)
            b.r = {}
        self.ops[eng].append((waits, fn, key, inc))


def build_program(layers, first, last):
    nc = bass.Bass("TRN2", target_bir_lowering=False)
    S = Sched()

    def dram(name, shape, kind="ExternalInput"):
        return nc.dram_tensor(name, list(shape), F32, kind=kind).ap()

    xT = dram("xT", [2, 128, NCH, TT])
    pT = dram("pT", [DEPTH, 2, 128, 2, TT])
    maskd = dram("mask", [128, 2])
    vecd = dram("vecs", [DEPTH, 128, NV])
    identd = dram("ident", [128, 128])
    w_in = dram("w_in", [DEPTH, D, 4608])
    w_a_out = dram("w_a_out", [DEPTH, 512, D])
    w_b_out = dram("w_b_out", [DEPTH, 512, D])
    w_o = dram("w_o", [DEPTH, D, D])
    w_up = dram("w_up", [DEPTH, D, 2 * DFF])
    w_down = dram("w_down", [DEPTH, DFF, D])
    w_ple = dram("w_ple", [DEPTH, 256, D])
    w_pg = dram("w_ple_gate", [DEPTH, D, D])
    if last:
        yT = dram("yT", [2, 128, NCH, OWN], kind="ExternalOutput")
    else:
        yT = dram("yT", [2, 128, NCH, TT], kind="ExternalOutput")

    def wview(w, l):
        return w[l].rearrange("(kc p) n -> p kc n", p=128)

    from contextlib import ExitStack

    with ExitStack() as es:
        def sb(name, shape, dt):
            return es.enter_context(nc.sbuf_tensor(name, list(shape), dt))

        x3 = sb("x3", [128, NCH, TT], F32)
        h3 = sb("h3", [128, NCH, TT], BF16)
        R = sb("R", [128, 11968], F32)
        a2_3 = sb("a2_3", [128, 4, TT], BF16)
        s3 = sb("s3", [128, 4, TT], BF16)
        sq3 = sb("sq3", [128, NCH, 512], BF16)
        stg = [sb("stg%d" % i, [128, 2 + TT], F32) for i in range(2)]
        NTMP = 6
        NLT = 4
        ltmps = [sb("ltmp%d" % i, [128, 512], F32) for i in range(NLT)]
        tmps = [sb("tmp%d" % i, [128, 512], F32) for i in range(NTMP)]
        ring = [sb("ring%d" % i, [128, SLOT_ELEMS], BF16) for i in range(RING_SLOTS)]
        p3s = [sb("p3_%d" % i, [128, 2, TT], BF16) for i in range(2)]
        vec = sb("vec", [128, DEPTH, NV], F32)
        msk = sb("msk", [128, 2], F32)
        ones = sb("ones", [128, 128], BF16)
        ident = sb("ident_sb", [128, 128], BF16)
        ps = es.enter_context(nc.psum_tensor("ps", [128, 8, 512], F32))

        Rb = R[:, :].bitcast(BF16)
        A_W = 30 + TT
        a3 = Rb[:, 0:4 * A_W].rearrange("p (c t) -> p c t", c=4)
        o = 4 * A_W
        dg = [Rb[:, o + i * 3968: o + (i + 1) * 3968].rearrange("p (k m) -> p k m", k=31) for i in range(2)]
        o += 2 * 3968
        csb3 = R[:, 6204:6204 + 4 * TT].rearrange("p (c t) -> p c t", c=4)
        assert 6204 + 4 * TT <= 11968
        mg3 = Rb[:, 0:NCH * TT].rearrange("p (c t) -> p c t", c=NCH)
        f3 = Rb[:, 0:NFF * TT].rearrange("p (c t) -> p c t", c=NFF)
        ost = [R[:, i * 3072:(i + 1) * 3072].rearrange("p (c t) -> p c t", c=NCH) for i in range(3)]

        NT = len(TILES)
        xb = [[Buf("x%d_%d" % (c, n)) for n in range(NT)] for c in range(NCH)]
        hb = [[Buf("h%d_%d" % (c, n)) for n in range(NT)] for c in range(NCH)]
        ab = [[Buf("a") for n in range(NT)] for j in range(4)]
        apad = Buf("apad")
        dgb = [Buf("dg0", True), Buf("dg1", True)]
        csbb = [[Buf("c") for n in range(NT)] for j in range(4)]
        a2b = [[Buf("a2") for n in range(NT)] for j in range(4)]
        sbb = [[Buf("s") for n in range(NT)] for j in range(4)]
        mgb = [[Buf("mg") for n in range(NT)] for m in range(NCH)]
        fb = [[Buf("f") for n in range(NT)] for j in range(NFF)]
        ostb = [Buf("ost0", True), Buf("ost1", True), Buf("ost2", True)]
        RA, RB = Buf("RA", True), Buf("RB", True)
        RCd, RCa, RDd, RDa = Buf("RCd", True), Buf("RCa", True), Buf("RDd", True), Buf("RDa", True)
        sqb = Buf("sq", True)
        stgb = [[Buf("stg") for n in range(NT)] for i in range(2)]
        tmpb = [Buf("tmp%d" % i) for i in range(NTMP)]
        ltmpb = [Buf("ltmp%d" % i) for i in range(NLT)]
        ringb = [Buf("ring%d" % i) for i in range(RING_SLOTS)]
        pbs = [Buf("p0"), Buf("p1")]
        vecb = Buf("vec")
        mskb = Buf("msk")
        onesb = Buf("ones")
        identb = Buf("ident")
        psb = [Buf("ps%d" % i) for i in range(8)]

        st = {"tmp": 0, "bank": 0, "ring": 0, "ltmp": 0, "p": 0}

        def ltmp():
            i = st["ltmp"]
            st["ltmp"] = (i + 1) % NLT
            return ltmps[i], ltmpb[i]

        def tmp():
            i = st["tmp"]
            st["tmp"] = (i + 1) % NTMP
            return tmps[i], tmpb[i]

        def bank():
            i = st["bank"]
            st["bank"] = (i + 1) % 8
            return i

        def load_slab(src, kc, cols):
            i = st["ring"]
            st["ring"] = (i + 1) % RING_SLOTS
            dst = ring[i][:, 0:kc * cols].rearrange("p (k c) -> p k c", k=kc)
            S.op("pool", lambda e, dst=dst, src=src: e.dma_start(out=dst, in_=src),
                 writes=[ringb[i]], sem="D_ring%d" % i, inc=16)
            return dst, ringb[i]

        def mm_group(bk, N, parts, reads):
            def fn(e, bk=bk, N=N, parts=parts):
                ins = None
                last_i = len(parts) - 1
                for i, (l, r) in enumerate(parts):
                    ins = e.matmul(ps[:, bk, 0:N], lhsT=l, rhs=r, start=(i == 0), stop=(i == last_i))
                return ins
            S.op("pe", fn, reads=reads, writes=[psb[bk]])

        def act(out, in_, func, reads, writes, bias=None, scale=None):
            kw = {}
            if bias is not None:
                kw["bias"] = bias
            if scale is not None:
                kw["scale"] = scale
            S.op("act", lambda e: e.activation(out=out, in_=in_, func=func, **kw), reads=reads, writes=writes)

        def tt(out, in0, in1, op, reads, writes):
            S.op("dve", lambda e: e.tensor_tensor(out=out, in0=in0, in1=in1, op=op), reads=reads, writes=writes)

        def ts(out, in0, s1, op0, reads, writes, s2=None, op1=None):
            if op1 is None:
                S.op("dve", lambda e: e.tensor_scalar(out=out, in0=in0, scalar1=s1, scalar2=None, op0=op0),
                     reads=reads, writes=writes)
            else:
                S.op("dve", lambda e: e.tensor_scalar(out=out, in0=in0, scalar1=s1, scalar2=s2, op0=op0, op1=op1),
                     reads=reads, writes=writes)

        def stt(out, in0, scalar, in1, op0, op1, reads, writes):
            S.op("dve", lambda e: e.scalar_tensor_tensor(out=out, in0=in0, scalar=scalar, in1=in1, op0=op0, op1=op1),
                 reads=reads, writes=writes)

        S.op("sp", lambda e: e.dma_start(out=vec[:, :, :], in_=vecd.rearrange("l p v -> p l v")),
             writes=[vecb], sem="D_vec", inc=16)
        S.op("sp", lambda e: e.dma_start(out=msk[:, :], in_=maskd),
             writes=[mskb], sem="D_msk", inc=16)
        S.op("pool", lambda e: e.dma_start(out=ident[:, :], in_=identd), writes=[identb], sem="D_misc2", inc=16)
        S.op("dve", lambda e: e.memset(ones[:, :], 1.0), writes=[onesb])
        for i in range(2):
            S.op("dve", lambda e, i=i: e.memset(stg[i][:, 0:2], 0.0), writes=[stgb[i][0]])

        def vcol(l, c):
            return vec[:, l, c:c + 1]

        def rstd_pre(n):
            off, N = TILES[n]
            S.op("act", lambda e: e.activation(out=sq3[:, :, 0:N], in_=x3[:, :, off:off + N], func=AF.Square),
                 reads=[xb[c][n] for c in range(NCH)], writes=[sqb])

        def rstd_post(n):
            off, N = TILES[n]
            bk = bank()
            mm_group(bk, N, [(ones[:, :], sq3[:, c, 0:N]) for c in range(NCH)], [sqb, onesb])
            ms, msb_ = tmp()
            ts(ms[:, 0:N], ps[:, bk, 0:N], 1.0 / D, ALU.mult, [psb[bk]], [msb_], s2=EPS, op1=ALU.add)
            sd, sdb = tmp()
            act(sd[:, 0:N], ms[:, 0:N], AF.Ln, [msb_], [sdb])
            rs, rsb = tmp()
            act(rs[:, 0:N], sd[:, 0:N], AF.Exp, [sdb], [rsb], scale=-0.5)
            return rs, rsb

        def rmsnorm_post(l, gcol, n):
            off, N = TILES[n]
            rs, rsb = rstd_post(n)
            for c in range(NCH):
                stt(h3[:, c, off:off + N], x3[:, c, off:off + N], vcol(l, gcol + c), rs[:, 0:N],
                    ALU.mult, ALU.mult, [xb[c][n], rsb, vecb], [hb[c][n]])

        def rmsnorm(l, gcol):
            for n in range(len(TILES)):
                rstd_pre(n)
                rmsnorm_post(l, gcol, n)

        def hreads(n):
            return [hb[c][n] for c in range(NCH)]

        def layer(l, ps_, pre_normed, after_ple):
            Win = wview(w_in, l)
            if last and l == layers[-1]:
                T_ = [(32, TILES[0][1] - 32)] + TILES[1:]
            else:
                T_ = TILES
            pi = st["p"]
            st["p"] = 1 - pi
            p3, pb = p3s[pi], pbs[pi]
            S.op("pool", lambda e: e.dma_start(out=p3[:, :, :], in_=pT[l, ps_]), writes=[pb], sem="D_p%d" % pi, inc=16)

            if not pre_normed:
                rmsnorm(l, V_GMIX)
            S.op("dve", lambda e: e.memset(a3[:, :, 0:30], 0.0), writes=[apad, RCd, RDd])
            for jp in range(2):
                Vs, Vb = load_slab(Win[:, :, 256 * jp:256 * jp + 256], NCH, 256)
                Gs, Gb = load_slab(Win[:, :, 512 + 256 * jp:512 + 256 * jp + 256], NCH, 256)
                for n, (off, N) in enumerate(T_):
                    for jj in range(2):
                        j = 2 * jp + jj
                        b1, b2 = bank(), bank()
                        mm_group(b1, N, [(Vs[:, k, jj * 128:(jj + 1) * 128], h3[:, k, off:off + N]) for k in range(NCH)],
                                 hreads(n) + [Vb])
                        mm_group(b2, N, [(Gs[:, k, jj * 128:(jj + 1) * 128], h3[:, k, off:off + N]) for k in range(NCH)],
                                 hreads(n) + [Gb])
                        sg, sgb = tmp()
                        act(sg[:, 0:N], ps[:, b2, 0:N], AF.Sigmoid, [psb[b2]], [sgb])
                        dst = a3[:, j, 30 + off:30 + off + N]
                        tt(dst, ps[:, b1, 0:N], sg[:, 0:N], ALU.mult, [psb[b1], sgb], [ab[j][n], RCd, RDd])
                        if n == 0:
                            hv = a3[:, j, 30:30 + HALO]
                            ts(hv, hv, msk[:, ps_:ps_ + 1], ALU.mult, [ab[j][n], mskb], [ab[j][n], RCd, RDd])
            def dg_build(j):
                di = j % 2
                for k in range(31):
                    ts(dg[di][:, k, :], ident[:, :], vcol(l, V_CAW + j * 31 + k), ALU.mult,
                       [identb, vecb], [dgb[di], RCd, RDd])

            def conv_unit(j, n):
                    di = j % 2
                    off, N = T_[n]
                    bk = bank()
                    rd = [dgb[di], apad] + [ab[j][q] for q in range(max(0, n - 1), n + 1)]
                    mm_group(bk, N, [(dg[di][:, k, :], a3[:, j, off + k:off + k + N]) for k in range(31)], rd + [RA])
                    act(csb3[:, j, off:off + N], ps[:, bk, 0:N], AF.Identity, [psb[bk], vecb], [csbb[j][n], RCa, RDa],
                        bias=vcol(l, V_CAB + j))
            lnst = {}

            def ln_pre(n):
                off, N = T_[n]
                crd = [csbb[j][n] for j in range(4)]
                S.op("act", lambda e, off=off, N=N: e.activation(out=sq3[:, 0:4, 0:N], in_=csb3[:, :, off:off + N],
                                                                 func=AF.Identity), reads=crd + [RA], writes=[sqb])
                S.op("act", lambda e, off=off, N=N: e.activation(out=sq3[:, 4:8, 0:N], in_=csb3[:, :, off:off + N],
                                                                 func=AF.Square), reads=crd + [RA], writes=[sqb])

            def ln_stats(n):
                off, N = T_[n]
                b1, b2 = bank(), bank()
                mm_group(b1, N, [(ones[:, :], sq3[:, c, 0:N]) for c in range(4)], [sqb, onesb])
                mm_group(b2, N, [(ones[:, :], sq3[:, 4 + c, 0:N]) for c in range(4)], [sqb, onesb])
                mu, mub = ltmp()
                ts(mu[:, 0:N], ps[:, b1, 0:N], 1.0 / 512, ALU.mult, [psb[b1]], [mub])
                m2, m2b = tmp()
                tt(m2[:, 0:N], mu[:, 0:N], mu[:, 0:N], ALU.mult, [mub], [m2b])
                stt(m2[:, 0:N], ps[:, b2, 0:N], 1.0 / 512, m2[:, 0:N], ALU.mult, ALU.subtract, [psb[b2], m2b], [m2b])
                ts(m2[:, 0:N], m2[:, 0:N], EPS, ALU.add, [m2b], [m2b])
                act(m2[:, 0:N], m2[:, 0:N], AF.Ln, [m2b], [m2b])
                rs, rsb = ltmp()
                act(rs[:, 0:N], m2[:, 0:N], AF.Exp, [m2b], [rsb], scale=-0.5)
                lnst[n] = (mu, mub, rs, rsb)

            def ln_apply(n):
                off, N = T_[n]
                mu, mub, rs, rsb = lnst[n]
                for j in range(4):
                    xh, xhb = tmp()
                    tt(xh[:, 0:N], csb3[:, j, off:off + N], mu[:, 0:N], ALU.subtract, [csbb[j][n], mub, RA], [xhb])
                    tt(xh[:, 0:N], xh[:, 0:N], rs[:, 0:N], ALU.mult, [xhb, rsb], [xhb])
                    act(a2_3[:, j, off:off + N], xh[:, 0:N], AF.Silu, [xhb, vecb], [a2b[j][n]],
                        bias=vcol(l, V_LNB + j), scale=vcol(l, V_LNG + j))
            zb_slabs = {}

            def zb_load(jp):
                zb_slabs[jp] = (load_slab(Win[:, :, 1024 + 256 * jp:1024 + 256 * jp + 256], NCH, 256),
                                load_slab(Win[:, :, 1536 + 256 * jp:1536 + 256 * jp + 256], NCH, 256),
                                load_slab(Win[:, :, 2048 + 256 * jp:2048 + 256 * jp + 256], NCH, 256))

            def zb_tile(jp, n):
                    (Bs, Bb), (Cs, Cb), (Vs, Vb) = zb_slabs[jp]
                    off, N = T_[n]
                    for jj in range(2):
                        j = 2 * jp + jj
                        si = j % 2
                        cv = stg[si]
                        bb, bc, bv = bank(), bank(), bank()
                        sl = slice(jj * 128, (jj + 1) * 128)
                        mm_group(bc, N, [(Cs[:, k, sl], h3[:, k, off:off + N]) for k in range(NCH)], hreads(n) + [Cb])
                        mm_group(bv, N, [(Vs[:, k, sl], h3[:, k, off:off + N]) for k in range(NCH)], hreads(n) + [Vb])
                        mm_group(bb, N, [(Bs[:, k, sl], h3[:, k, off:off + N]) for k in range(NCH)], hreads(n) + [Bb])
                        vt, vtb = tmp()
                        act(vt[:, 0:N], ps[:, bv, 0:N], AF.Identity, [psb[bv]], [vtb])
                        dst = cv[:, 2 + off:2 + off + N]
                        tt(dst, ps[:, bc, 0:N], vt[:, 0:N], ALU.mult, [psb[bc], vtb], [stgb[si][n]])
                        if n == 0:
                            hv = cv[:, 2:2 + HALO]
                            ts(hv, hv, msk[:, ps_:ps_ + 1], ALU.mult, [stgb[si][n], mskb], [stgb[si][n]])
                        acc, accb = tmp()
                        crd = [stgb[si][q] for q in range(max(0, n - 1), n + 1)] + [stgb[si][0], vecb]
                        act(acc[:, 0:N], cv[:, 2 + off:2 + off + N], AF.Identity, crd, [accb],
                            scale=vcol(l, V_CBW + j * 3 + 2))
                        stt(acc[:, 0:N], cv[:, 1 + off:1 + off + N], vcol(l, V_CBW + j * 3 + 1), acc[:, 0:N],
                            ALU.mult, ALU.add, crd + [accb], [accb])
                        stt(acc[:, 0:N], cv[:, off:off + N], vcol(l, V_CBW + j * 3 + 0), acc[:, 0:N],
                            ALU.mult, ALU.add, crd + [accb], [accb])
                        tt(s3[:, j, off:off + N], ps[:, bb, 0:N], acc[:, 0:N], ALU.mult, [psb[bb], accb], [sbb[j][n]])

            zb_load(0)
            zb_load(1)
            order = [("dg", 0), ("zb", 0, 0), ("cv", 0, 0), ("dg", 1), ("zb", 0, 1), ("cv", 0, 1), ("cv", 0, 2),
                     ("cv", 1, 0), ("dg", 2), ("zb", 0, 2), ("cv", 1, 1), ("cv", 1, 2), ("dg", 3), ("zb", 1, 0),
                     ("cv", 2, 0), ("zb", 1, 1), ("cv", 2, 1), ("cv", 3, 0), ("lnpre", 0), ("zb", 1, 2), ("cv", 3, 1),
                     ("lnstats", 0), ("lnpre", 1), ("cv", 2, 2), ("lnapply", 0), ("lnstats", 1), ("cv", 3, 2),
                     ("lnpre", 2), ("lnapply", 1), ("lnstats", 2), ("lnapply", 2)]
            fns = {"zb": zb_tile, "dg": dg_build, "cv": conv_unit, "lnpre": ln_pre, "lnstats": ln_stats,
                   "lnapply": ln_apply}
            for it in order:
                fns[it[0]](*it[1:])
            WA = wview(w_a_out, l)
            WB = wview(w_b_out, l)
            for mp in range(4):
                c0 = 256 * mp
                As, Ab_ = load_slab(WA[:, :, c0:c0 + 256], 4, 256)
                Bs, Bb = load_slab(WB[:, :, c0:c0 + 256], 4, 256)
                GAs, GAb = load_slab(Win[:, :, 2560 + c0:2560 + c0 + 256], NCH, 256)
                GBs, GBb = load_slab(Win[:, :, 3584 + c0:3584 + c0 + 256], NCH, 256)
                for n, (off, N) in enumerate(T_):
                    for mm in range(2):
                        m = 2 * mp + mm
                        sl = slice(mm * 128, (mm + 1) * 128)
                        bya, bga, byb, bgb = bank(), bank(), bank(), bank()
                        mm_group(bga, N, [(GAs[:, k, sl], h3[:, k, off:off + N]) for k in range(NCH)], hreads(n) + [GAb])
                        mm_group(bgb, N, [(GBs[:, k, sl], h3[:, k, off:off + N]) for k in range(NCH)], hreads(n) + [GBb])
                        mm_group(bya, N, [(As[:, k, sl], a2_3[:, k, off:off + N]) for k in range(4)],
                                 [a2b[k][n] for k in range(4)] + [Ab_])
                        mm_group(byb, N, [(Bs[:, k, sl], s3[:, k, off:off + N]) for k in range(4)],
                                 [sbb[k][n] for k in range(4)] + [Bb])
                        ga, gab = tmp()
                        act(ga[:, 0:N], ps[:, bga, 0:N], AF.Sigmoid, [psb[bga], vecb], [gab], bias=vcol(l, V_BGA + m))
                        gb, gbb = tmp()
                        act(gb[:, 0:N], ps[:, bgb, 0:N], AF.Sigmoid, [psb[bgb], vecb], [gbb], bias=vcol(l, V_BGB + m))
                        tt(ga[:, 0:N], ps[:, bya, 0:N], ga[:, 0:N], ALU.mult, [psb[bya], gab], [gab])
                        tt(gb[:, 0:N], ps[:, byb, 0:N], gb[:, 0:N], ALU.mult, [psb[byb], gbb], [gbb])
                        tt(mg3[:, m, off:off + N], ga[:, 0:N], gb[:, 0:N], ALU.add, [gab, gbb], [mgb[m][n], RA])
            WO = wview(w_o, l)
            wo_slabs = [load_slab(WO[:, :, 256 * mp:256 * mp + 256], NCH, 256) for mp in range(4)]
            for n, (off, N) in enumerate(T_):
                for m in range(NCH):
                    if m == 4 and n > 0:
                        rmsnorm_post(l, V_GFFN, n - 1)
                    if True:
                        Os, Ob = wo_slabs[m // 2]
                        sl = slice((m % 2) * 128, (m % 2 + 1) * 128)
                        bk = bank()
                        mm_group(bk, N, [(Os[:, k, sl], mg3[:, k, off:off + N]) for k in range(NCH)],
                                 [mgb[k][n] for k in range(NCH)] + [Ob, RB])
                        tt(x3[:, m, off:off + N], ps[:, bk, 0:N], x3[:, m, off:off + N], ALU.add,
                           [psb[bk], xb[m][n]], [xb[m][n]])
                rstd_pre(n)
            rmsnorm_post(l, V_GFFN, NT - 1)

            WU = wview(w_up, l)
            pend = []

            def ffn_tail(j, n, acc, accb, bv):
                off, N = T_[n]
                gl, glb = tmp()
                act(gl[:, 0:N], acc[:, 0:N], AF.Gelu_apprx_tanh, [accb], [glb])
                tt(f3[:, j, off:off + N], gl[:, 0:N], ps[:, bv, 0:N], ALU.mult, [glb, psb[bv]], [fb[j][n], RB])

            for jp in range(11):
                Gs, Gb = load_slab(WU[:, :, 256 * jp:256 * jp + 256], NCH, 256)
                Vs, Vb = load_slab(WU[:, :, DFF + 256 * jp:DFF + 256 * jp + 256], NCH, 256)
                for n, (off, N) in enumerate(T_):
                    for jj in range(2):
                        j = 2 * jp + jj
                        si = j % 2
                        ub = stg[si]
                        sl = slice(jj * 128, (jj + 1) * 128)
                        bg, bv = bank(), bank()
                        mm_group(bg, N, [(Gs[:, k, sl], h3[:, k, off:off + N]) for k in range(NCH)], hreads(n) + [Gb])
                        mm_group(bv, N, [(Vs[:, k, sl], h3[:, k, off:off + N]) for k in range(NCH)], hreads(n) + [Vb])
                        dst = ub[:, 2 + off:2 + off + N]
                        act(dst, ps[:, bg, 0:N], AF.Identity, [psb[bg]], [stgb[si][n]])
                        acc, accb = tmp()
                        act(acc[:, 0:N], ps[:, bg, 0:N], AF.Identity, [psb[bg], vecb], [accb],
                            bias=vcol(l, V_CFB + j), scale=vcol(l, V_CFW + j * 3 + 2))
                        if n == 0:
                            hv = ub[:, 2:2 + HALO]
                            ts(hv, hv, msk[:, ps_:ps_ + 1], ALU.mult, [stgb[si][n], mskb], [stgb[si][n]])
                        crd = [stgb[si][q] for q in range(max(0, n - 1), n + 1)] + [stgb[si][0], vecb]
                        stt(acc[:, 0:N], ub[:, 1 + off:1 + off + N], vcol(l, V_CFW + j * 3 + 1), acc[:, 0:N],
                            ALU.mult, ALU.add, crd + [accb], [accb])
                        stt(acc[:, 0:N], ub[:, off:off + N], vcol(l, V_CFW + j * 3 + 0), acc[:, 0:N],
                            ALU.mult, ALU.add, crd + [accb], [accb])
                        if pend:
                            ffn_tail(*pend.pop())
                        pend.append((j, n, acc, accb, bv))
            ffn_tail(*pend.pop())
            WD = wview(w_down, l)
            for mp in range(4):
                c0 = 256 * mp
                parts_k = [(0, 8), (8, 8), (16, 6)]
                slabs = [load_slab(WD[:, k0:k0 + kn, c0:c0 + 256], kn, 256) for (k0, kn) in parts_k]
                for n, (off, N) in enumerate(T_):
                    for mm in range(2):
                        m = 2 * mp + mm
                        sl = slice(mm * 128, (mm + 1) * 128)
                        bk = bank()
                        parts = []
                        for (k0, kn), (Ds, Db) in zip(parts_k, slabs):
                            parts += [(Ds[:, k, sl], f3[:, k0 + k, off:off + N]) for k in range(kn)]
                        mm_group(bk, N, parts, [fb[k][n] for k in range(NFF)] + [s_[1] for s_ in slabs] + [RCd, RCa])
                        tt(x3[:, m, off:off + N], ps[:, bk, 0:N], x3[:, m, off:off + N], ALU.add,
                           [psb[bk], xb[m][n]], [xb[m][n]])
                    if mp == 3:
                        if n > 0:
                            rmsnorm_post(l, V_GPLE, n - 1)
                        rstd_pre(n)
            rmsnorm_post(l, V_GPLE, NT - 1)

            WG = wview(w_pg, l)
            WP = wview(w_ple, l)
            pg_slabs = [load_slab(WG[:, :, 256 * mp:256 * mp + 256], NCH, 256) for mp in range(4)]
            wp_slabs = [load_slab(WP[:, :, 512 * q:512 * q + 512], 2, 512) for q in range(2)]
            for n, (off, N) in enumerate(T_):
                for m in range(NCH):
                    if m == 4 and n > 0 and after_ple is not None:
                        after_ple(n - 1)
                    if True:
                        Gs, Gb = pg_slabs[m // 2]
                        sl = slice((m % 2) * 128, (m % 2 + 1) * 128)
                        Ps0, Pb_ = wp_slabs[m // 4]
                        Ps = Ps0[:, :, (m % 4) * 128 - (m % 2) * 128:]
                        bg, bp = bank(), bank()
                        mm_group(bg, N, [(Gs[:, k, sl], h3[:, k, off:off + N]) for k in range(NCH)], hreads(n) + [Gb])
                        mm_group(bp, N, [(Ps[:, k, sl], p3[:, k, off:off + N]) for k in range(2)], [pb, Pb_])
                        pg, pgb = tmp()
                        act(pg[:, 0:N], ps[:, bg, 0:N], AF.Sigmoid, [psb[bg]], [pgb])
                        tt(pg[:, 0:N], ps[:, bp, 0:N], pg[:, 0:N], ALU.mult, [psb[bp], pgb], [pgb])
                        tt(x3[:, m, off:off + N], pg[:, 0:N], x3[:, m, off:off + N], ALU.add,
                           [pgb, xb[m][n]], [xb[m][n]])
                if after_ple is not None:
                    rstd_pre(n)
            if after_ple is not None:
                after_ple(NT - 1)

        allx = [xb[c][n] for c in range(NCH) for n in range(NT)]
        def xload(ps_, n):
            off, N = TILES[n]
            S.op("sp", lambda e: e.dma_start(out=x3[:, :, off:off + N], in_=xT[ps_, :, :, off:off + N]),
                 writes=[xb[c][n] for c in range(NCH)], sem="D_x%d" % n, inc=16)

        def final_tile(ps_, n):
            off, N = TILES[n]
            rs, rsb = rstd_post(n)
            oi = n
            lo = max(off, HALO)
            No = off + N - lo
            for c in range(NCH):
                stt(ost[oi][:, c, 0:No], x3[:, c, lo:lo + No], vcol(0, V_GFIN + c), rs[:, lo - off:lo - off + No],
                    ALU.mult, ALU.mult, [xb[c][n], rsb, vecb], [ostb[oi], RCd, RCa])
            S.op("sp", lambda e: e.dma_start(out=yT[ps_, :, :, lo - HALO:lo - HALO + No], in_=ost[oi][:, :, 0:No]),
                 reads=[ostb[oi], RDd, RDa], sem="D_out%d" % oi, inc=16)
            if ps_ == 0:
                xload(1, n)

        for ps_ in range(2):
            if ps_ == 0 or not last:
                for n in range(NT):
                    xload(ps_, n)
            for li, l in enumerate(layers):
                if li + 1 < len(layers):
                    cb = (lambda n, nxt=layers[li + 1]: rmsnorm_post(nxt, V_GMIX, n))
                elif last and ps_ == 0:
                    def cb(n, ps_=ps_):
                        final_tile(ps_, n)
                        if n >= 1:
                            rstd_pre(n - 1)
                            rmsnorm_post(layers[0], V_GMIX, n - 1)
                elif last:
                    cb = (lambda n, ps_=ps_: final_tile(ps_, n))
                else:
                    cb = None
                pre = li > 0
                if last and ps_ == 1 and li == 0:
                    rstd_pre(NT - 1)
                    rmsnorm_post(layers[0], V_GMIX, NT - 1)
                    pre = True
                layer(l, ps_, pre, cb)
            if not last:
                S.op("sp", lambda e, ps_=ps_: e.dma_start(out=yT[ps_], in_=x3[:, :, :]), reads=allx, sem="D_out0", inc=16)
        out_totals = [(k, v) for k, v in S.count.items() if k.startswith("D_out")]

        sem_names = sorted(S.count.keys())
        sems = {k: es.enter_context(nc.semaphore(k)) for k in sem_names}
        block = es.enter_context(nc.Block())

        def replay(eng_name, e, tail=None):
            for waits, fn, key, inc in S.ops[eng_name]:
                for k, v in waits:
                    e.wait_ge(sems[k], v)
                ins = fn(e)
                ins.then_inc(sems[key], inc)
            if tail is not None:
                tail(e)

        @block.tensor
        def _(e):
            replay("pe", e)

        @block.scalar
        def _(e):
            replay("act", e)

        @block.vector
        def _(e):
            replay("dve", e)

        @block.gpsimd
        def _(e):
            replay("pool", e)

        @block.sync
        def _(e):
            replay("sp", e, tail=lambda e: [e.wait_ge(sems[k], v) for k, v in out_totals])

    return nc


def _pack_vecs(inp):
    def cols(v):
        v = np.asarray(v, np.float32)
        return v.reshape(-1, 128).T

    out = np.zeros((DEPTH, 128, NV), np.float32)
    for l in range(DEPTH):
        o = out[l]
        o[:, V_GMIX:V_GMIX + 8] = cols(inp["g_mix"][l])
        o[:, V_BGA:V_BGA + 8] = cols(inp["b_gate"][l][:D])
        o[:, V_BGB:V_BGB + 8] = cols(inp["b_gate"][l][D:])
        caw = np.asarray(inp["conv_a_w"][l], np.float32)
        for j in range(4):
            o[:, V_CAW + j * 31:V_CAW + (j + 1) * 31] = caw[:, j * 128:(j + 1) * 128].T
        o[:, V_CAB:V_CAB + 4] = cols(inp["conv_a_b"][l])
        o[:, V_LNG:V_LNG + 4] = cols(inp["ln_a_g"][l])
        o[:, V_LNB:V_LNB + 4] = cols(inp["ln_a_b"][l])
        cbw = np.asarray(inp["conv_b_w"][l], np.float32)
        for j in range(4):
            o[:, V_CBW + j * 3:V_CBW + (j + 1) * 3] = cbw[:, j * 128:(j + 1) * 128].T
        o[:, V_GFFN:V_GFFN + 8] = cols(inp["g_ffn"][l])
        cfw = np.asarray(inp["conv_f_w"][l], np.float32)
        for j in range(NFF):
            o[:, V_CFW + j * 3:V_CFW + (j + 1) * 3] = cfw[:, j * 128:(j + 1) * 128].T
        o[:, V_CFB:V_CFB + NFF] = cols(inp["conv_f_b"][l])
        o[:, V_GPLE:V_GPLE + 8] = cols(inp["g_ple"][l])
        o[:, V_GFIN:V_GFIN + 8] = cols(inp["g_final"])
    return out


def _vc(core, ps_):
    v = 2 * core + ps_
    return v // 8, (v % 8) * OWN


_PROGS = {}


def _prog(layers, first, last):
    key = (tuple(layers), first, last)
    if key not in _PROGS:
        _PROGS[key] = build_program(list(layers), first, last)
    return _PROGS[key]


FUSED = True


def make_in_maps(inputs):
    inp = {k: np.asarray(v) for k, v in inputs.items()}
    x = inp["x"].astype(np.float32, copy=False)
    p = inp["p"].astype(np.float32, copy=False)
    vecs = _pack_vecs(inp)
    ident = np.eye(128, dtype=np.float32)
    wnames = ["w_in", "w_a_out", "w_b_out", "w_o", "w_up", "w_down", "w_ple", "w_ple_gate"]
    shared = {k: np.ascontiguousarray(inp[k], dtype=np.float32) for k in wnames}
    shared["vecs"] = vecs
    shared["ident"] = ident

    in_maps = []
    for c in range(NCORES):
        xT = np.zeros((2, 128, NCH, TT), np.float32)
        pTa = np.zeros((DEPTH, 2, 128, 2, TT), np.float32)
        mask = np.ones((128, 2), np.float32)
        for ps_ in range(2):
            b, t0 = _vc(c, ps_)
            lo = t0 - HALO
            if lo < 0:
                mask[:, ps_] = 0.0
                xs = x[b, 0:t0 + OWN]
                xT[ps_, :, :, HALO:] = xs.T.reshape(NCH, 128, OWN).transpose(1, 0, 2)
                for l in range(DEPTH):
                    pTa[l, ps_, :, :, HALO:] = p[l, b, 0:t0 + OWN].T.reshape(2, 128, OWN).transpose(1, 0, 2)
            else:
                xs = x[b, lo:t0 + OWN]
                xT[ps_] = xs.T.reshape(NCH, 128, TT).transpose(1, 0, 2)
                for l in range(DEPTH):
                    pTa[l, ps_] = p[l, b, lo:t0 + OWN].T.reshape(2, 128, TT).transpose(1, 0, 2)
        m = dict(shared)
        m["xT"] = xT
        m["pT"] = pTa
        m["mask"] = mask
        in_maps.append(m)
    return in_maps


def kernel(**inputs):
    in_maps = make_in_maps(inputs)
    if FUSED:
        nc = _prog(range(DEPTH), True, True)
        res = run_bass_kernel_spmd(nc, in_maps, core_ids=list(range(NCORES)))
        outs = [r["yT"] for r in res.results]
    else:
        outs = None
        for l in range(DEPTH):
            nc = _prog([l], l == 0, l == DEPTH - 1)
            res = run_bass_kernel_spmd(nc, in_maps, core_ids=list(range(NCORES)))
            outs = [r["yT"] for r in res.results]
            if l < DEPTH - 1:
                for c in range(NCORES):
                    in_maps[c]["xT"] = np.ascontiguousarray(outs[c])

    y = np.zeros((2, 8192, D), np.float32)
    for c in range(NCORES):
        for ps_ in range(2):
            b, t0 = _vc(c, ps_)
            yt = outs[c][ps_]
            y[b, t0:t0 + OWN, :] = yt.transpose(2, 1, 0).reshape(OWN, D)
    return y
```

```python
import numpy as np
import concourse.bass as bass
import concourse.mybir as mybir
from concourse.bass_utils import run_bass_kernel_spmd

F32 = mybir.dt.float32
BF16 = mybir.dt.bfloat16
AF = mybir.ActivationFunctionType
ALU = mybir.AluOpType

D = 1024
NCH = 8
HALO = 64
OWN = 1024
TT = HALO + OWN
TILES = [(0, 384), (384, 352), (736, 352)]
DEPTH = 2
DFF = 2816
NFF = 22
EPS = 1e-6
NCORES = 8

V_GMIX = 0
V_BGA = 8
V_BGB = 16
V_CAW = 24
V_CAB = 148
V_LNG = 152
V_LNB = 156
V_CBW = 160
V_GFFN = 172
V_CFW = 180
V_CFB = 246
V_GPLE = 268
V_GFIN = 276
NV = 284

RING_SLOTS = 10
STRICT_SAME_ENGINE = True
SLOT_ELEMS = 2048


class Buf:
    __slots__ = ("name", "w", "r", "coarse")

    def __init__(self, name, coarse=False):
        self.name = name
        self.w = None
        self.r = {}
        self.coarse = coarse


class Sched:
    ENGS = ("pe", "act", "dve", "pool", "sp")

    def __init__(self):
        self.ops = {e: [] for e in self.ENGS}
        self.count = {}
        self.clk = {e: {} for e in self.ENGS}
        self.done = {}

    def op(self, eng, fn, reads=(), writes=(), sem=None, inc=1):
        key = sem if sem is not None else "E_" + eng
        clk = self.clk[eng]
        cand = {}

        def need(k, v):
            if v > clk.get(k, 0) and v > cand.get(k, 0):
                cand[k] = v

        for b in reads:
            if b.w is not None:
                need(*b.w)
        strict = STRICT_SAME_ENGINE and eng in ("act", "dve")
        for b in writes:
            st_b = strict and not b.coarse
            if b.w is not None and (st_b or b.w[0] != key):
                need(*b.w)
            for k, v in b.r.items():
                if st_b or k != key:
                    need(k, v)
        waits = []
        for k in sorted(cand, key=lambda k: (k == key, k)):
            v = cand[k]
            if clk.get(k, 0) >= v:
                continue
            waits.append((k, v))
            for kk, vv in self.done[(k, v)].items():
                if vv > clk.get(kk, 0):
                    clk[kk] = vv
        val = self.count.get(key, 0) + inc
        self.count[key] = val
        d = dict(clk)
        d[key] = val
        self.done[(key, val)] = d
        for b in reads:
            b.r[key] = val
        for b in writes:
            b.w = (key, val)
            b.r = {}
        self.ops[eng].append((waits, fn, key, inc))


def build_program(layers, first, last):
    nc = bass.Bass("TRN2", target_bir_lowering=False)
    S = Sched()

    def dram(name, shape, kind="ExternalInput"):
        return nc.dram_tensor(name, list(shape), F32, kind=kind).ap()

    xT = dram("xT", [2, 128, NCH, TT])
    pT = dram("pT", [DEPTH, 2, 128, 2, TT])
    maskd = dram("mask", [128, 2])
    vecd = dram("vecs", [DEPTH, 128, NV])
    identd = dram("ident", [128, 128])
    w_in = dram("w_in", [DEPTH, D, 4608])
    w_a_out = dram("w_a_out", [DEPTH, 512, D])
    w_b_out = dram("w_b_out", [DEPTH, 512, D])
    w_o = dram("w_o", [DEPTH, D, D])
    w_up = dram("w_up", [DEPTH, D, 2 * DFF])
    w_down = dram("w_down", [DEPTH, DFF, D])
    w_ple = dram("w_ple", [DEPTH, 256, D])
    w_pg = dram("w_ple_gate", [DEPTH, D, D])
    if last:
        yT = dram("yT", [2, 128, NCH, OWN], kind="ExternalOutput")
    else:
        yT = dram("yT", [2, 128, NCH, TT], kind="ExternalOutput")

    def wview(w, l):
        return w[l].rearrange("(kc p) n -> p kc n", p=128)

    from contextlib import ExitStack

    with ExitStack() as es:
        def sb(name, shape, dt):
            return es.enter_context(nc.sbuf_tensor(name, list(shape), dt))

        x3 = sb("x3", [128, NCH, TT], F32)
        h3 = sb("h3", [128, NCH, TT], BF16)
        R = sb("R", [128, 11968], F32)
        a2_3 = sb("a2_3", [128, 4, TT], BF16)
        s3 = sb("s3", [128, 4, TT], BF16)
        sq3 = sb("sq3", [128, NCH, 512], BF16)
        stg = [sb("stg%d" % i, [128, 2 + TT], F32) for i in range(2)]
        NTMP = 6
        NLT = 4
        ltmps = [sb("ltmp%d" % i, [128, 512], F32) for i in range(NLT)]
        tmps = [sb("tmp%d" % i, [128, 512], F32) for i in range(NTMP)]
        ring = [sb("ring%d" % i, [128, SLOT_ELEMS], BF16) for i in range(RING_SLOTS)]
        p3s = [sb("p3_%d" % i, [128, 2, TT], BF16) for i in range(2)]
        vec = sb("vec", [128, DEPTH, NV], F32)
        msk = sb("msk", [128, 2], F32)
        ones = sb("ones", [128, 128], BF16)
        ident = sb("ident_sb", [128, 128], BF16)
        ps = es.enter_context(nc.psum_tensor("ps", [128, 8, 512], F32))

        Rb = R[:, :].bitcast(BF16)
        A_W = 30 + TT
        a3 = Rb[:, 0:4 * A_W].rearrange("p (c t) -> p c t", c=4)
        o = 4 * A_W
        dg = [Rb[:, o + i * 3968: o + (i + 1) * 3968].rearrange("p (k m) -> p k m", k=31) for i in range(2)]
        o += 2 * 3968
        csb3 = R[:, 6204:6204 + 4 * TT].rearrange("p (c t) -> p c t", c=4)
        assert 6204 + 4 * TT <= 11968
        mg3 = Rb[:, 0:NCH * TT].rearrange("p (c t) -> p c t", c=NCH)
        f3 = Rb[:, 0:NFF * TT].rearrange("p (c t) -> p c t", c=NFF)
        ost = [R[:, i * 3072:(i + 1) * 3072].rearrange("p (c t) -> p c t", c=NCH) for i in range(3)]

        NT = len(TILES)
        xb = [[Buf("x%d_%d" % (c, n)) for n in range(NT)] for c in range(NCH)]
        hb = [[Buf("h%d_%d" % (c, n)) for n in range(NT)] for c in range(NCH)]
        ab = [[Buf("a") for n in range(NT)] for j in range(4)]
        apad = Buf("apad")
        dgb = [Buf("dg0", True), Buf("dg1", True)]
        csbb = [[Buf("c") for n in range(NT)] for j in range(4)]
        a2b = [[Buf("a2") for n in range(NT)] for j in range(4)]
        sbb = [[Buf("s") for n in range(NT)] for j in range(4)]
        mgb = [[Buf("mg") for n in range(NT)] for m in range(NCH)]
        fb = [[Buf("f") for n in range(NT)] for j in range(NFF)]
        ostb = [Buf("ost0", True), Buf("ost1", True), Buf("ost2", True)]
        RA, RB = Buf("RA", True), Buf("RB", True)
        RCd, RCa, RDd, RDa = Buf("RCd", True), Buf("RCa", True), Buf("RDd", True), Buf("RDa", True)
        sqb = Buf("sq", True)
        stgb = [[Buf("stg") for n in range(NT)] for i in range(2)]
        tmpb = [Buf("tmp%d" % i) for i in range(NTMP)]
        ltmpb = [Buf("ltmp%d" % i) for i in range(NLT)]
        ringb = [Buf("ring%d" % i) for i in range(RING_SLOTS)]
        pbs = [Buf("p0"), Buf("p1")]
        vecb = Buf("vec")
        mskb = Buf("msk")
        onesb = Buf("ones")
        identb = Buf("ident")
        psb = [Buf("ps%d" % i) for i in range(8)]

        st = {"tmp": 0, "bank": 0, "ring": 0, "ltmp": 0, "p": 0}

        def ltmp():
            i = st["ltmp"]
            st["ltmp"] = (i + 1) % NLT
            return ltmps[i], ltmpb[i]

        def tmp():
            i = st["tmp"]
            st["tmp"] = (i + 1) % NTMP
            return tmps[i], tmpb[i]

        def bank():
            i = st["bank"]
            st["bank"] = (i + 1) % 8
            return i

        def load_slab(src, kc, cols):
            i = st["ring"]
            st["ring"] = (i + 1) % RING_SLOTS
            dst = ring[i][:, 0:kc * cols].rearrange("p (k c) -> p k c", k=kc)
            S.op("pool", lambda e, dst=dst, src=src: e.dma_start(out=dst, in_=src),
                 writes=[ringb[i]], sem="D_ring%d" % i, inc=16)
            return dst, ringb[i]

        def mm_group(bk, N, parts, reads):
            def fn(e, bk=bk, N=N, parts=parts):
                ins = None
                last_i = len(parts) - 1
                for i, (l, r) in enumerate(parts):
                    ins = e.matmul(ps[:, bk, 0:N], lhsT=l, rhs=r, start=(i == 0), stop=(i == last_i))
                return ins
            S.op("pe", fn, reads=reads, writes=[psb[bk]])

        def act(out, in_, func, reads, writes, bias=None, scale=None):
            kw = {}
            if bias is not None:
                kw["bias"] = bias
            if scale is not None:
                kw["scale"] = scale
            S.op("act", lambda e: e.activation(out=out, in_=in_, func=func, **kw), reads=reads, writes=writes)

        def tt(out, in0, in1, op, reads, writes):
            S.op("dve", lambda e: e.tensor_tensor(out=out, in0=in0, in1=in1, op=op), reads=reads, writes=writes)

        def ts(out, in0, s1, op0, reads, writes, s2=None, op1=None):
            if op1 is None:
                S.op("dve", lambda e: e.tensor_scalar(out=out, in0=in0, scalar1=s1, scalar2=None, op0=op0),
                     reads=reads, writes=writes)
            else:
                S.op("dve", lambda e: e.tensor_scalar(out=out, in0=in0, scalar1=s1, scalar2=s2, op0=op0, op1=op1),
                     reads=reads, writes=writes)

        def stt(out, in0, scalar, in1, op0, op1, reads, writes):
            S.op("dve", lambda e: e.scalar_tensor_tensor(out=out, in0=in0, scalar=scalar, in1=in1, op0=op0, op1=op1),
                 reads=reads, writes=writes)

        S.op("sp", lambda e: e.dma_start(out=vec[:, :, :], in_=vecd.rearrange("l p v -> p l v")),
             writes=[vecb], sem="D_vec", inc=16)
        S.op("sp", lambda e: e.dma_start(out=msk[:, :], in_=maskd),
             writes=[mskb], sem="D_msk", inc=16)
        S.op("pool", lambda e: e.dma_start(out=ident[:, :], in_=identd), writes=[identb], sem="D_misc2", inc=16)
        S.op("dve", lambda e: e.memset(ones[:, :], 1.0), writes=[onesb])
        for i in range(2):
            S.op("dve", lambda e, i=i: e.memset(stg[i][:, 0:2], 0.0), writes=[stgb[i][0]])

        def vcol(l, c):
            return vec[:, l, c:c + 1]

        def rstd_pre(n):
            off, N = TILES[n]
            S.op("act", lambda e: e.activation(out=sq3[:, :, 0:N], in_=x3[:, :, off:off + N], func=AF.Square),
                 reads=[xb[c][n] for c in range(NCH)], writes=[sqb])

        def rstd_post(n):
            off, N = TILES[n]
            bk = bank()
            mm_group(bk, N, [(ones[:, :], sq3[:, c, 0:N]) for c in range(NCH)], [sqb, onesb])
            ms, msb_ = tmp()
            ts(ms[:, 0:N], ps[:, bk, 0:N], 1.0 / D, ALU.mult, [psb[bk]], [msb_], s2=EPS, op1=ALU.add)
            sd, sdb = tmp()
            act(sd[:, 0:N], ms[:, 0:N], AF.Ln, [msb_], [sdb])
            rs, rsb = tmp()
            act(rs[:, 0:N], sd[:, 0:N], AF.Exp, [sdb], [rsb], scale=-0.5)
            return rs, rsb

        def rmsnorm_post(l, gcol, n):
            off, N = TILES[n]
            rs, rsb = rstd_post(n)
            for c in range(NCH):
                stt(h3[:, c, off:off + N], x3[:, c, off:off + N], vcol(l, gcol + c), rs[:, 0:N],
                    ALU.mult, ALU.mult, [xb[c][n], rsb, vecb], [hb[c][n]])

        def rmsnorm(l, gcol):
            for n in range(len(TILES)):
                rstd_pre(n)
                rmsnorm_post(l, gcol, n)

        def hreads(n):
            return [hb[c][n] for c in range(NCH)]

        def layer(l, ps_, pre_normed, after_ple):
            Win = wview(w_in, l)
            if last and l == layers[-1]:
                T_ = [(32, TILES[0][1] - 32)] + TILES[1:]
            else:
                T_ = TILES
            if last and l == layers[-1]:
                T2_ = [(HALO, TILES[0][0] + TILES[0][1] - HALO)] + TILES[1:]
            else:
                T2_ = T_
            pi = st["p"]
            st["p"] = 1 - pi
            p3, pb = p3s[pi], pbs[pi]
            S.op("pool", lambda e: e.dma_start(out=p3[:, :, :], in_=pT[l, ps_]), writes=[pb], sem="D_p%d" % pi, inc=16)

            if not pre_normed:
                rmsnorm(l, V_GMIX)
            S.op("dve", lambda e: e.memset(a3[:, :, 0:30], 0.0), writes=[apad, RCd, RDd])
            for jp in range(2):
                Vs, Vb = load_slab(Win[:, :, 256 * jp:256 * jp + 256], NCH, 256)
                Gs, Gb = load_slab(Win[:, :, 512 + 256 * jp:512 + 256 * jp + 256], NCH, 256)
                for n, (off, N) in enumerate(T_):
                    for jj in range(2):
                        j = 2 * jp + jj
                        b1, b2 = bank(), bank()
                        mm_group(b1, N, [(Vs[:, k, jj * 128:(jj + 1) * 128], h3[:, k, off:off + N]) for k in range(NCH)],
                                 hreads(n) + [Vb])
                        mm_group(b2, N, [(Gs[:, k, jj * 128:(jj + 1) * 128], h3[:, k, off:off + N]) for k in range(NCH)],
                                 hreads(n) + [Gb])
                        sg, sgb = tmp()
                        act(sg[:, 0:N], ps[:, b2, 0:N], AF.Sigmoid, [psb[b2]], [sgb])
                        dst = a3[:, j, 30 + off:30 + off + N]
                        tt(dst, ps[:, b1, 0:N], sg[:, 0:N], ALU.mult, [psb[b1], sgb], [ab[j][n], RCd, RDd])
                        if n == 0:
                            hv = a3[:, j, 30:30 + HALO]
                            ts(hv, hv, msk[:, ps_:ps_ + 1], ALU.mult, [ab[j][n], mskb], [ab[j][n], RCd, RDd])
            def dg_build(j):
                di = j % 2
                for k in range(31):
                    ts(dg[di][:, k, :], ident[:, :], vcol(l, V_CAW + j * 31 + k), ALU.mult,
                       [identb, vecb], [dgb[di], RCd, RDd])

            def conv_unit(j, n):
                    di = j % 2
                    off, N = T_[n]
                    bk = bank()
                    rd = [dgb[di], apad] + [ab[j][q] for q in range(max(0, n - 1), n + 1)]
                    mm_group(bk, N, [(dg[di][:, k, :], a3[:, j, off + k:off + k + N]) for k in range(31)], rd + [RA])
                    act(csb3[:, j, off:off + N], ps[:, bk, 0:N], AF.Identity, [psb[bk], vecb], [csbb[j][n], RCa, RDa],
                        bias=vcol(l, V_CAB + j))
            lnst = {}

            def ln_pre(n):
                off, N = T_[n]
                crd = [csbb[j][n] for j in range(4)]
                S.op("act", lambda e, off=off, N=N: e.activation(out=sq3[:, 0:4, 0:N], in_=csb3[:, :, off:off + N],
                                                                 func=AF.Identity), reads=crd + [RA], writes=[sqb])
                S.op("act", lambda e, off=off, N=N: e.activation(out=sq3[:, 4:8, 0:N], in_=csb3[:, :, off:off + N],
                                                                 func=AF.Square), reads=crd + [RA], writes=[sqb])

            def ln_stats(n):
                off, N = T_[n]
                b1, b2 = bank(), bank()
                mm_group(b1, N, [(ones[:, :], sq3[:, c, 0:N]) for c in range(4)], [sqb, onesb])
                mm_group(b2, N, [(ones[:, :], sq3[:, 4 + c, 0:N]) for c in range(4)], [sqb, onesb])
                mu, mub = ltmp()
                ts(mu[:, 0:N], ps[:, b1, 0:N], 1.0 / 512, ALU.mult, [psb[b1]], [mub])
                m2, m2b = tmp()
                tt(m2[:, 0:N], mu[:, 0:N], mu[:, 0:N], ALU.mult, [mub], [m2b])
                stt(m2[:, 0:N], ps[:, b2, 0:N], 1.0 / 512, m2[:, 0:N], ALU.mult, ALU.subtract, [psb[b2], m2b], [m2b])
                ts(m2[:, 0:N], m2[:, 0:N], EPS, ALU.add, [m2b], [m2b])
                act(m2[:, 0:N], m2[:, 0:N], AF.Ln, [m2b], [m2b])
                rs, rsb = ltmp()
                act(rs[:, 0:N], m2[:, 0:N], AF.Exp, [m2b], [rsb], scale=-0.5)
                lnst[n] = (mu, mub, rs, rsb)

            def ln_apply(n):
                off, N = T_[n]
                mu, mub, rs, rsb = lnst[n]
                for j in range(4):
                    xh, xhb = tmp()
                    tt(xh[:, 0:N], csb3[:, j, off:off + N], mu[:, 0:N], ALU.subtract, [csbb[j][n], mub, RA], [xhb])
                    tt(xh[:, 0:N], xh[:, 0:N], rs[:, 0:N], ALU.mult, [xhb, rsb], [xhb])
                    act(a2_3[:, j, off:off + N], xh[:, 0:N], AF.Silu, [xhb, vecb], [a2b[j][n]],
                        bias=vcol(l, V_LNB + j), scale=vcol(l, V_LNG + j))
            zb_slabs = {}

            def zb_load(jp):
                zb_slabs[jp] = (load_slab(Win[:, :, 1024 + 256 * jp:1024 + 256 * jp + 256], NCH, 256),
                                load_slab(Win[:, :, 1536 + 256 * jp:1536 + 256 * jp + 256], NCH, 256),
                                load_slab(Win[:, :, 2048 + 256 * jp:2048 + 256 * jp + 256], NCH, 256))

            def zb_tile(jp, n):
                    (Bs, Bb), (Cs, Cb), (Vs, Vb) = zb_slabs[jp]
                    off, N = T_[n]
                    for jj in range(2):
                        j = 2 * jp + jj
                        si = j % 2
                        cv = stg[si]
                        bb, bc, bv = bank(), bank(), bank()
                        sl = slice(jj * 128, (jj + 1) * 128)
                        mm_group(bc, N, [(Cs[:, k, sl], h3[:, k, off:off + N]) for k in range(NCH)], hreads(n) + [Cb])
                        mm_group(bv, N, [(Vs[:, k, sl], h3[:, k, off:off + N]) for k in range(NCH)], hreads(n) + [Vb])
                        mm_group(bb, N, [(Bs[:, k, sl], h3[:, k, off:off + N]) for k in range(NCH)], hreads(n) + [Bb])
                        vt, vtb = tmp()
                        act(vt[:, 0:N], ps[:, bv, 0:N], AF.Identity, [psb[bv]], [vtb])
                        dst = cv[:, 2 + off:2 + off + N]
                        tt(dst, ps[:, bc, 0:N], vt[:, 0:N], ALU.mult, [psb[bc], vtb], [stgb[si][n]])
                        if n == 0:
                            hv = cv[:, 2:2 + HALO]
                            ts(hv, hv, msk[:, ps_:ps_ + 1], ALU.mult, [stgb[si][n], mskb], [stgb[si][n]])
                        acc, accb = tmp()
                        crd = [stgb[si][q] for q in range(max(0, n - 1), n + 1)] + [stgb[si][0], vecb]
                        act(acc[:, 0:N], cv[:, 2 + off:2 + off + N], AF.Identity, crd, [accb],
                            scale=vcol(l, V_CBW + j * 3 + 2))
                        stt(acc[:, 0:N], cv[:, 1 + off:1 + off + N], vcol(l, V_CBW + j * 3 + 1), acc[:, 0:N],
                            ALU.mult, ALU.add, crd + [accb], [accb])
                        stt(acc[:, 0:N], cv[:, off:off + N], vcol(l, V_CBW + j * 3 + 0), acc[:, 0:N],
                            ALU.mult, ALU.add, crd + [accb], [accb])
                        tt(s3[:, j, off:off + N], ps[:, bb, 0:N], acc[:, 0:N], ALU.mult, [psb[bb], accb], [sbb[j][n]])

            zb_load(0)
            zb_load(1)
            order = [("dg", 0), ("zb", 0, 0), ("cv", 0, 0), ("dg", 1), ("zb", 0, 1), ("cv", 0, 1), ("cv", 0, 2),
                     ("cv", 1, 0), ("dg", 2), ("zb", 0, 2), ("cv", 1, 1), ("cv", 1, 2), ("dg", 3), ("zb", 1, 0),
                     ("cv", 2, 0), ("zb", 1, 1), ("cv", 2, 1), ("cv", 3, 0), ("lnpre", 0), ("zb", 1, 2), ("cv", 3, 1),
                     ("lnstats", 0), ("lnpre", 1), ("cv", 2, 2), ("lnapply", 0), ("lnstats", 1), ("cv", 3, 2),
                     ("lnpre", 2), ("lnapply", 1), ("lnstats", 2), ("lnapply", 2)]
            fns = {"zb": zb_tile, "dg": dg_build, "cv": conv_unit, "lnpre": ln_pre, "lnstats": ln_stats,
                   "lnapply": ln_apply}
            for it in order:
                fns[it[0]](*it[1:])
            WA = wview(w_a_out, l)
            WB = wview(w_b_out, l)
            for mp in range(4):
                c0 = 256 * mp
                As, Ab_ = load_slab(WA[:, :, c0:c0 + 256], 4, 256)
                Bs, Bb = load_slab(WB[:, :, c0:c0 + 256], 4, 256)
                GAs, GAb = load_slab(Win[:, :, 2560 + c0:2560 + c0 + 256], NCH, 256)
                GBs, GBb = load_slab(Win[:, :, 3584 + c0:3584 + c0 + 256], NCH, 256)
                for n, (off, N) in enumerate(T_):
                    for mm in range(2):
                        m = 2 * mp + mm
                        sl = slice(mm * 128, (mm + 1) * 128)
                        bya, bga, byb, bgb = bank(), bank(), bank(), bank()
                        mm_group(bga, N, [(GAs[:, k, sl], h3[:, k, off:off + N]) for k in range(NCH)], hreads(n) + [GAb])
                        mm_group(bgb, N, [(GBs[:, k, sl], h3[:, k, off:off + N]) for k in range(NCH)], hreads(n) + [GBb])
                        mm_group(bya, N, [(As[:, k, sl], a2_3[:, k, off:off + N]) for k in range(4)],
                                 [a2b[k][n] for k in range(4)] + [Ab_])
                        mm_group(byb, N, [(Bs[:, k, sl], s3[:, k, off:off + N]) for k in range(4)],
                                 [sbb[k][n] for k in range(4)] + [Bb])
                        ga, gab = tmp()
                        act(ga[:, 0:N], ps[:, bga, 0:N], AF.Sigmoid, [psb[bga], vecb], [gab], bias=vcol(l, V_BGA + m))
                        gb, gbb = tmp()
                        act(gb[:, 0:N], ps[:, bgb, 0:N], AF.Sigmoid, [psb[bgb], vecb], [gbb], bias=vcol(l, V_BGB + m))
                        tt(ga[:, 0:N], ps[:, bya, 0:N], ga[:, 0:N], ALU.mult, [psb[bya], gab], [gab])
                        tt(gb[:, 0:N], ps[:, byb, 0:N], gb[:, 0:N], ALU.mult, [psb[byb], gbb], [gbb])
                        tt(mg3[:, m, off:off + N], ga[:, 0:N], gb[:, 0:N], ALU.add, [gab, gbb], [mgb[m][n], RA])
            WO = wview(w_o, l)
            wo_slabs = [load_slab(WO[:, :, 256 * mp:256 * mp + 256], NCH, 256) for mp in range(4)]
            for n, (off, N) in enumerate(T_):
                for m in range(NCH):
                    if m == 4 and n > 0:
                        rmsnorm_post(l, V_GFFN, n - 1)
                    if True:
                        Os, Ob = wo_slabs[m // 2]
                        sl = slice((m % 2) * 128, (m % 2 + 1) * 128)
                        bk = bank()
                        mm_group(bk, N, [(Os[:, k, sl], mg3[:, k, off:off + N]) for k in range(NCH)],
                                 [mgb[k][n] for k in range(NCH)] + [Ob, RB])
                        tt(x3[:, m, off:off + N], ps[:, bk, 0:N], x3[:, m, off:off + N], ALU.add,
                           [psb[bk], xb[m][n]], [xb[m][n]])
                rstd_pre(n)
            rmsnorm_post(l, V_GFFN, NT - 1)

            WU = wview(w_up, l)
            pend = []

            def ffn_tail(j, n, acc, accb, bv):
                off, N = T_[n]
                gl, glb = tmp()
                act(gl[:, 0:N], acc[:, 0:N], AF.Gelu_apprx_tanh, [accb], [glb])
                tt(f3[:, j, off:off + N], gl[:, 0:N], ps[:, bv, 0:N], ALU.mult, [glb, psb[bv]], [fb[j][n], RB])

            for jp in range(11):
                Gs, Gb = load_slab(WU[:, :, 256 * jp:256 * jp + 256], NCH, 256)
                Vs, Vb = load_slab(WU[:, :, DFF + 256 * jp:DFF + 256 * jp + 256], NCH, 256)
                for n, (off, N) in enumerate(T_):
                    for jj in range(2):
                        j = 2 * jp + jj
                        si = j % 2
                        ub = stg[si]
                        sl = slice(jj * 128, (jj + 1) * 128)
                        bg, bv = bank(), bank()
                        mm_group(bg, N, [(Gs[:, k, sl], h3[:, k, off:off + N]) for k in range(NCH)], hreads(n) + [Gb])
                        mm_group(bv, N, [(Vs[:, k, sl], h3[:, k, off:off + N]) for k in range(NCH)], hreads(n) + [Vb])
                        dst = ub[:, 2 + off:2 + off + N]
                        act(dst, ps[:, bg, 0:N], AF.Identity, [psb[bg]], [stgb[si][n]])
                        acc, accb = tmp()
                        act(acc[:, 0:N], ps[:, bg, 0:N], AF.Identity, [psb[bg], vecb], [accb],
                            bias=vcol(l, V_CFB + j), scale=vcol(l, V_CFW + j * 3 + 2))
                        if n == 0:
                            hv = ub[:, 2:2 + HALO]
                            ts(hv, hv, msk[:, ps_:ps_ + 1], ALU.mult, [stgb[si][n], mskb], [stgb[si][n]])
                        crd = [stgb[si][q] for q in range(max(0, n - 1), n + 1)] + [stgb[si][0], vecb]
                        stt(acc[:, 0:N], ub[:, 1 + off:1 + off + N], vcol(l, V_CFW + j * 3 + 1), acc[:, 0:N],
                            ALU.mult, ALU.add, crd + [accb], [accb])
                        stt(acc[:, 0:N], ub[:, off:off + N], vcol(l, V_CFW + j * 3 + 0), acc[:, 0:N],
                            ALU.mult, ALU.add, crd + [accb], [accb])
                        if pend:
                            ffn_tail(*pend.pop())
                        pend.append((j, n, acc, accb, bv))
            ffn_tail(*pend.pop())
            WD = wview(w_down, l)
            for mp in range(4):
                c0 = 256 * mp
                parts_k = [(0, 8), (8, 8), (16, 6)]
                slabs = [load_slab(WD[:, k0:k0 + kn, c0:c0 + 256], kn, 256) for (k0, kn) in parts_k]
                for n, (off, N) in enumerate(T2_):
                    for mm in range(2):
                        m = 2 * mp + mm
                        sl = slice(mm * 128, (mm + 1) * 128)
                        bk = bank()
                        parts = []
                        for (k0, kn), (Ds, Db) in zip(parts_k, slabs):
                            parts += [(Ds[:, k, sl], f3[:, k0 + k, off:off + N]) for k in range(kn)]
                        mm_group(bk, N, parts, [fb[k][n] for k in range(NFF)] + [s_[1] for s_ in slabs] + [RCd, RCa])
                        tt(x3[:, m, off:off + N], ps[:, bk, 0:N], x3[:, m, off:off + N], ALU.add,
                           [psb[bk], xb[m][n]], [xb[m][n]])
                    if mp == 3:
                        if n > 0:
                            rmsnorm_post(l, V_GPLE, n - 1)
                        rstd_pre(n)
            rmsnorm_post(l, V_GPLE, NT - 1)

            WG = wview(w_pg, l)
            WP = wview(w_ple, l)
            pg_slabs = [load_slab(WG[:, :, 256 * mp:256 * mp + 256], NCH, 256) for mp in range(4)]
            wp_slabs = [load_slab(WP[:, :, 512 * q:512 * q + 512], 2, 512) for q in range(2)]
            for n, (off, N) in enumerate(T2_):
                for m in range(NCH):
                    if m == 4 and n > 0 and after_ple is not None:
                        after_ple(n - 1)
                    if True:
                        Gs, Gb = pg_slabs[m // 2]
                        sl = slice((m % 2) * 128, (m % 2 + 1) * 128)
                        Ps0, Pb_ = wp_slabs[m // 4]
                        Ps = Ps0[:, :, (m % 4) * 128 - (m % 2) * 128:]
                        bg, bp = bank(), bank()
                        mm_group(bg, N, [(Gs[:, k, sl], h3[:, k, off:off + N]) for k in range(NCH)], hreads(n) + [Gb])
                        mm_group(bp, N, [(Ps[:, k, sl], p3[:, k, off:off + N]) for k in range(2)], [pb, Pb_])
                        pg, pgb = tmp()
                        act(pg[:, 0:N], ps[:, bg, 0:N], AF.Sigmoid, [psb[bg]], [pgb])
                        tt(pg[:, 0:N], ps[:, bp, 0:N], pg[:, 0:N], ALU.mult, [psb[bp], pgb], [pgb])
                        tt(x3[:, m, off:off + N], pg[:, 0:N], x3[:, m, off:off + N], ALU.add,
                           [pgb, xb[m][n]], [xb[m][n]])
                if after_ple is not None:
                    rstd_pre(n)
            if after_ple is not None:
                after_ple(NT - 1)

        allx = [xb[c][n] for c in range(NCH) for n in range(NT)]
        def xload(ps_, n):
            off, N = TILES[n]
            S.op("sp", lambda e: e.dma_start(out=x3[:, :, off:off + N], in_=xT[ps_, :, :, off:off + N]),
                 writes=[xb[c][n] for c in range(NCH)], sem="D_x%d" % n, inc=16)

        def final_tile(ps_, n):
            off, N = TILES[n]
            rs, rsb = rstd_post(n)
            oi = n
            lo = max(off, HALO)
            No = off + N - lo
            for c in range(NCH):
                stt(ost[oi][:, c, 0:No], x3[:, c, lo:lo + No], vcol(0, V_GFIN + c), rs[:, lo - off:lo - off + No],
                    ALU.mult, ALU.mult, [xb[c][n], rsb, vecb], [ostb[oi], RCd, RCa])
            S.op("sp", lambda e: e.dma_start(out=yT[ps_, :, :, lo - HALO:lo - HALO + No], in_=ost[oi][:, :, 0:No]),
                 reads=[ostb[oi], RDd, RDa], sem="D_out%d" % oi, inc=16)
            if ps_ == 0:
                xload(1, n)

        for ps_ in range(2):
            if ps_ == 0 or not last:
                for n in range(NT):
                    xload(ps_, n)
            for li, l in enumerate(layers):
                if li + 1 < len(layers):
                    cb = (lambda n, nxt=layers[li + 1]: rmsnorm_post(nxt, V_GMIX, n))
                elif last and ps_ == 0:
                    def cb(n, ps_=ps_):
                        final_tile(ps_, n)
                        if n >= 1:
                            rstd_pre(n - 1)
                            rmsnorm_post(layers[0], V_GMIX, n - 1)
                elif last:
                    cb = (lambda n, ps_=ps_: final_tile(ps_, n))
                else:
                    cb = None
                pre = li > 0
                if last and ps_ == 1 and li == 0:
                    rstd_pre(NT - 1)
                    rmsnorm_post(layers[0], V_GMIX, NT - 1)
                    pre = True
                layer(l, ps_, pre, cb)
            if not last:
                S.op("sp", lambda e, ps_=ps_: e.dma_start(out=yT[ps_], in_=x3[:, :, :]), reads=allx, sem="D_out0", inc=16)
        out_totals = [(k, v) for k, v in S.count.items() if k.startswith("D_out")]

        sem_names = sorted(S.count.keys())
        sems = {k: es.enter_context(nc.semaphore(k)) for k in sem_names}
        block = es.enter_context(nc.Block())

        def replay(eng_name, e, tail=None):
            for waits, fn, key, inc in S.ops[eng_name]:
                for k, v in waits:
                    e.wait_ge(sems[k], v)
                ins = fn(e)
                ins.then_inc(sems[key], inc)
            if tail is not None:
                tail(e)

        @block.tensor
        def _(e):
            replay("pe", e)

        @block.scalar
        def _(e):
            replay("act", e)

        @block.vector
        def _(e):
            replay("dve", e)

        @block.gpsimd
        def _(e):
            replay("pool", e)

        @block.sync
        def _(e):
            replay("sp", e, tail=lambda e: [e.wait_ge(sems[k], v) for k, v in out_totals])

    return nc


def _pack_vecs(inp):
    def cols(v):
        v = np.asarray(v, np.float32)
        return v.reshape(-1, 128).T

    out = np.zeros((DEPTH, 128, NV), np.float32)
    for l in range(DEPTH):
        o = out[l]
        o[:, V_GMIX:V_GMIX + 8] = cols(inp["g_mix"][l])
        o[:, V_BGA:V_BGA + 8] = cols(inp["b_gate"][l][:D])
        o[:, V_BGB:V_BGB + 8] = cols(inp["b_gate"][l][D:])
        caw = np.asarray(inp["conv_a_w"][l], np.float32)
        for j in range(4):
            o[:, V_CAW + j * 31:V_CAW + (j + 1) * 31] = caw[:, j * 128:(j + 1) * 128].T
        o[:, V_CAB:V_CAB + 4] = cols(inp["conv_a_b"][l])
        o[:, V_LNG:V_LNG + 4] = cols(inp["ln_a_g"][l])
        o[:, V_LNB:V_LNB + 4] = cols(inp["ln_a_b"][l])
        cbw = np.asarray(inp["conv_b_w"][l], np.float32)
        for j in range(4):
            o[:, V_CBW + j * 3:V_CBW + (j + 1) * 3] = cbw[:, j * 128:(j + 1) * 128].T
        o[:, V_GFFN:V_GFFN + 8] = cols(inp["g_ffn"][l])
        cfw = np.asarray(inp["conv_f_w"][l], np.float32)
        for j in range(NFF):
            o[:, V_CFW + j * 3:V_CFW + (j + 1) * 3] = cfw[:, j * 128:(j + 1) * 128].T
        o[:, V_CFB:V_CFB + NFF] = cols(inp["conv_f_b"][l])
        o[:, V_GPLE:V_GPLE + 8] = cols(inp["g_ple"][l])
        o[:, V_GFIN:V_GFIN + 8] = cols(inp["g_final"])
    return out


def _vc(core, ps_):
    v = 2 * core + ps_
    return v // 8, (v % 8) * OWN


_PROGS = {}


def _prog(layers, first, last):
    key = (tuple(layers), first, last)
    if key not in _PROGS:
        _PROGS[key] = build_program(list(layers), first, last)
    return _PROGS[key]


FUSED = True


def make_in_maps(inputs):
    inp = {k: np.asarray(v) for k, v in inputs.items()}
    x = inp["x"].astype(np.float32, copy=False)
    p = inp["p"].astype(np.float32, copy=False)
    vecs = _pack_vecs(inp)
    ident = np.eye(128, dtype=np.float32)
    wnames = ["w_in", "w_a_out", "w_b_out", "w_o", "w_up", "w_down", "w_ple", "w_ple_gate"]
    shared = {k: np.ascontiguousarray(inp[k], dtype=np.float32) for k in wnames}
    shared["vecs"] = vecs
    shared["ident"] = ident

    in_maps = []
    for c in range(NCORES):
        xT = np.zeros((2, 128, NCH, TT), np.float32)
        pTa = np.zeros((DEPTH, 2, 128, 2, TT), np.float32)
        mask = np.ones((128, 2), np.float32)
        for ps_ in range(2):
            b, t0 = _vc(c, ps_)
            lo = t0 - HALO
            if lo < 0:
                mask[:, ps_] = 0.0
                xs = x[b, 0:t0 + OWN]
                xT[ps_, :, :, HALO:] = xs.T.reshape(NCH, 128, OWN).transpose(1, 0, 2)
                for l in range(DEPTH):
                    pTa[l, ps_, :, :, HALO:] = p[l, b, 0:t0 + OWN].T.reshape(2, 128, OWN).transpose(1, 0, 2)
            else:
                xs = x[b, lo:t0 + OWN]
                xT[ps_] = xs.T.reshape(NCH, 128, TT).transpose(1, 0, 2)
                for l in range(DEPTH):
                    pTa[l, ps_] = p[l, b, lo:t0 + OWN].T.reshape(2, 128, TT).transpose(1, 0, 2)
        m = dict(shared)
        m["xT"] = xT
        m["pT"] = pTa
        m["mask"] = mask
        in_maps.append(m)
    return in_maps


def kernel(**inputs):
    in_maps = make_in_maps(inputs)
    if FUSED:
        nc = _prog(range(DEPTH), True, True)
        res = run_bass_kernel_spmd(nc, in_maps, core_ids=list(range(NCORES)))
        outs = [r["yT"] for r in res.results]
    else:
        outs = None
        for l in range(DEPTH):
            nc = _prog([l], l == 0, l == DEPTH - 1)
            res = run_bass_kernel_spmd(nc, in_maps, core_ids=list(range(NCORES)))
            outs = [r["yT"] for r in res.results]
            if l < DEPTH - 1:
                for c in range(NCORES):
                    in_maps[c]["xT"] = np.ascontiguousarray(outs[c])

    y = np.zeros((2, 8192, D), np.float32)
    for c in range(NCORES):
        for ps_ in range(2):
            b, t0 = _vc(c, ps_)
            yt = outs[c][ps_]
            y[b, t0:t0 + OWN, :] = yt.transpose(2, 1, 0).reshape(OWN, D)
    return y
```

```python
import numpy as np
import concourse.bass as bass
import concourse.mybir as mybir
from concourse.bass_utils import run_bass_kernel_spmd

F32 = mybir.dt.float32
BF16 = mybir.dt.bfloat16
AF = mybir.ActivationFunctionType
ALU = mybir.AluOpType

D = 1024
NCH = 8
HALO = 64
OWN = 1024
TT = HALO + OWN
TILES = [(0, 384), (384, 352), (736, 352)]
DEPTH = 2
DFF = 2816
NFF = 22
EPS = 1e-6
NCORES = 8

V_GMIX = 0
V_BGA = 8
V_BGB = 16
V_CAW = 24
V_CAB = 148
V_LNG = 152
V_LNB = 156
V_CBW = 160
V_GFFN = 172
V_CFW = 180
V_CFB = 246
V_GPLE = 268
V_GFIN = 276
V_EPS = 284
NV = 285

RING_SLOTS = 10
STRICT_SAME_ENGINE = True
SLOT_ELEMS = 2048


class Buf:
    __slots__ = ("name", "w", "r", "coarse")

    def __init__(self, name, coarse=False):
        self.name = name
        self.w = None
        self.r = {}
        self.coarse = coarse


class Sched:
    ENGS = ("pe", "act", "dve", "pool", "sp")

    def __init__(self):
        self.ops = {e: [] for e in self.ENGS}
        self.count = {}
        self.clk = {e: {} for e in self.ENGS}
        self.done = {}

    def op(self, eng, fn, reads=(), writes=(), sem=None, inc=1):
        key = sem if sem is not None else "E_" + eng
        clk = self.clk[eng]
        cand = {}

        def need(k, v):
            if v > clk.get(k, 0) and v > cand.get(k, 0):
                cand[k] = v

        for b in reads:
            if b.w is not None:
                need(*b.w)
        strict = STRICT_SAME_ENGINE and eng in ("act", "dve")
        for b in writes:
            st_b = strict and not b.coarse
            if b.w is not None and (st_b or b.w[0] != key):
                need(*b.w)
            for k, v in b.r.items():
                if st_b or k != key:
                    need(k, v)
        waits = []
        for k in sorted(cand, key=lambda k: (k == key, k)):
            v = cand[k]
            if clk.get(k, 0) >= v:
                continue
            waits.append((k, v))
            for kk, vv in self.done[(k, v)].items():
                if vv > clk.get(kk, 0):
                    clk[kk] = vv
        val = self.count.get(key, 0) + inc
        self.count[key] = val
        d = dict(clk)
        d[key] = val
        self.done[(key, val)] = d
        for b in reads:
            b.r[key] = val
        for b in writes:
            b.w = (key, val)
            b.r = {}
        self.ops[eng].append((waits, fn, key, inc))


def build_program(layers, first, last):
    nc = bass.Bass("TRN2", target_bir_lowering=False)
    S = Sched()

    def dram(name, shape, kind="ExternalInput"):
        return nc.dram_tensor(name, list(shape), F32, kind=kind).ap()

    xT = dram("xT", [2, 128, NCH, TT])
    pT = dram("pT", [DEPTH, 2, 128, 2, TT])
    maskd = dram("mask", [128, 2])
    vecd = dram("vecs", [DEPTH, 128, NV])
    identd = dram("ident", [128, 128])
    w_in = dram("w_in", [DEPTH, D, 4608])
    w_a_out = dram("w_a_out", [DEPTH, 512, D])
    w_b_out = dram("w_b_out", [DEPTH, 512, D])
    w_o = dram("w_o", [DEPTH, D, D])
    w_up = dram("w_up", [DEPTH, D, 2 * DFF])
    w_down = dram("w_down", [DEPTH, DFF, D])
    w_ple = dram("w_ple", [DEPTH, 256, D])
    w_pg = dram("w_ple_gate", [DEPTH, D, D])
    if last:
        yT = dram("yT", [2, 128, NCH, OWN], kind="ExternalOutput")
    else:
        yT = dram("yT", [2, 128, NCH, TT], kind="ExternalOutput")

    def wview(w, l):
        return w[l].rearrange("(kc p) n -> p kc n", p=128)

    from contextlib import ExitStack

    with ExitStack() as es:
        def sb(name, shape, dt):
            return es.enter_context(nc.sbuf_tensor(name, list(shape), dt))

        x3 = sb("x3", [128, NCH, TT], F32)
        h3 = sb("h3", [128, NCH, TT], BF16)
        R = sb("R", [128, 11968], F32)
        a2_3 = sb("a2_3", [128, 4, TT], BF16)
        s3 = sb("s3", [128, 4, TT], BF16)
        sq3 = sb("sq3", [128, NCH, 512], BF16)
        stg = [sb("stg%d" % i, [128, 2 + TT], F32) for i in range(2)]
        NTMP = 6
        NLT = 4
        ltmps = [sb("ltmp%d" % i, [128, 512], F32) for i in range(NLT)]
        tmps = [sb("tmp%d" % i, [128, 512], F32) for i in range(NTMP)]
        ring = [sb("ring%d" % i, [128, SLOT_ELEMS], BF16) for i in range(RING_SLOTS)]
        p3s = [sb("p3_%d" % i, [128, 2, TT], BF16) for i in range(2)]
        vec = sb("vec", [128, DEPTH, NV], F32)
        msk = sb("msk", [128, 2], F32)
        ones = sb("ones", [128, 128], BF16)
        ident = sb("ident_sb", [128, 128], BF16)
        ps = es.enter_context(nc.psum_tensor("ps", [128, 8, 512], F32))

        Rb = R[:, :].bitcast(BF16)
        A_W = 30 + TT
        a3 = Rb[:, 0:4 * A_W].rearrange("p (c t) -> p c t", c=4)
        o = 4 * A_W
        dg = [Rb[:, o + i * 3968: o + (i + 1) * 3968].rearrange("p (k m) -> p k m", k=31) for i in range(2)]
        o += 2 * 3968
        csb3 = R[:, 6204:6204 + 4 * TT].rearrange("p (c t) -> p c t", c=4)
        assert 6204 + 4 * TT <= 11968
        mg3 = Rb[:, 0:NCH * TT].rearrange("p (c t) -> p c t", c=NCH)
        f3 = Rb[:, 0:NFF * TT].rearrange("p (c t) -> p c t", c=NFF)
        ost = [R[:, i * 3072:(i + 1) * 3072].rearrange("p (c t) -> p c t", c=NCH) for i in range(3)]

        NT = len(TILES)
        xb = [[Buf("x%d_%d" % (c, n)) for n in range(NT)] for c in range(NCH)]
        hb = [[Buf("h%d_%d" % (c, n)) for n in range(NT)] for c in range(NCH)]
        ab = [[Buf("a") for n in range(NT)] for j in range(4)]
        apad = Buf("apad")
        dgb = [Buf("dg0", True), Buf("dg1", True)]
        csbb = [[Buf("c") for n in range(NT)] for j in range(4)]
        a2b = [[Buf("a2") for n in range(NT)] for j in range(4)]
        sbb = [[Buf("s") for n in range(NT)] for j in range(4)]
        mgb = [[Buf("mg") for n in range(NT)] for m in range(NCH)]
        fb = [[Buf("f") for n in range(NT)] for j in range(NFF)]
        ostb = [Buf("ost0", True), Buf("ost1", True), Buf("ost2", True)]
        RA, RB = Buf("RA", True), Buf("RB", True)
        RCd, RCa, RDd, RDa = Buf("RCd", True), Buf("RCa", True), Buf("RDd", True), Buf("RDa", True)
        sqb = Buf("sq", True)
        stgb = [[Buf("stg") for n in range(NT)] for i in range(2)]
        tmpb = [Buf("tmp%d" % i) for i in range(NTMP)]
        ltmpb = [Buf("ltmp%d" % i) for i in range(NLT)]
        ringb = [Buf("ring%d" % i) for i in range(RING_SLOTS)]
        pbs = [Buf("p0"), Buf("p1")]
        vecb = Buf("vec")
        mskb = Buf("msk")
        onesb = Buf("ones")
        identb = Buf("ident")
        psb = [Buf("ps%d" % i) for i in range(8)]

        st = {"tmp": 0, "bank": 0, "ring": 0, "ltmp": 0, "p": 0}

        def ltmp():
            i = st["ltmp"]
            st["ltmp"] = (i + 1) % NLT
            return ltmps[i], ltmpb[i]

        def tmp():
            i = st["tmp"]
            st["tmp"] = (i + 1) % NTMP
            return tmps[i], tmpb[i]

        def bank():
            i = st["bank"]
            st["bank"] = (i + 1) % 8
            return i

        def load_slab(src, kc, cols):
            i = st["ring"]
            st["ring"] = (i + 1) % RING_SLOTS
            dst = ring[i][:, 0:kc * cols].rearrange("p (k c) -> p k c", k=kc)
            S.op("pool", lambda e, dst=dst, src=src: e.dma_start(out=dst, in_=src),
                 writes=[ringb[i]], sem="D_ring%d" % i, inc=16)
            return dst, ringb[i]

        def mm_group(bk, N, parts, reads):
            def fn(e, bk=bk, N=N, parts=parts):
                ins = None
                last_i = len(parts) - 1
                for i, (l, r) in enumerate(parts):
                    ins = e.matmul(ps[:, bk, 0:N], lhsT=l, rhs=r, start=(i == 0), stop=(i == last_i))
                return ins
            S.op("pe", fn, reads=reads, writes=[psb[bk]])

        def act(out, in_, func, reads, writes, bias=None, scale=None):
            kw = {}
            if bias is not None:
                kw["bias"] = bias
            if scale is not None:
                kw["scale"] = scale
            S.op("act", lambda e: e.activation(out=out, in_=in_, func=func, **kw), reads=reads, writes=writes)

        def tt(out, in0, in1, op, reads, writes):
            S.op("dve", lambda e: e.tensor_tensor(out=out, in0=in0, in1=in1, op=op), reads=reads, writes=writes)

        def ts(out, in0, s1, op0, reads, writes, s2=None, op1=None):
            if op1 is None:
                S.op("dve", lambda e: e.tensor_scalar(out=out, in0=in0, scalar1=s1, scalar2=None, op0=op0),
                     reads=reads, writes=writes)
            else:
                S.op("dve", lambda e: e.tensor_scalar(out=out, in0=in0, scalar1=s1, scalar2=s2, op0=op0, op1=op1),
                     reads=reads, writes=writes)

        def stt(out, in0, scalar, in1, op0, op1, reads, writes):
            S.op("dve", lambda e: e.scalar_tensor_tensor(out=out, in0=in0, scalar=scalar, in1=in1, op0=op0, op1=op1),
                 reads=reads, writes=writes)

        S.op("sp", lambda e: e.dma_start(out=vec[:, :, :], in_=vecd.rearrange("l p v -> p l v")),
             writes=[vecb], sem="D_vec", inc=16)
        S.op("sp", lambda e: e.dma_start(out=msk[:, :], in_=maskd),
             writes=[mskb], sem="D_msk", inc=16)
        S.op("pool", lambda e: e.dma_start(out=ident[:, :], in_=identd), writes=[identb], sem="D_misc2", inc=16)
        S.op("dve", lambda e: e.memset(ones[:, :], 1.0), writes=[onesb])
        for i in range(2):
            S.op("dve", lambda e, i=i: e.memset(stg[i][:, 0:2], 0.0), writes=[stgb[i][0]])

        def vcol(l, c):
            return vec[:, l, c:c + 1]

        def rstd_pre(n):
            off, N = TILES[n]
            S.op("act", lambda e: e.activation(out=sq3[:, :, 0:N], in_=x3[:, :, off:off + N], func=AF.Square),
                 reads=[xb[c][n] for c in range(NCH)], writes=[sqb])

        def rstd_post(n):
            off, N = TILES[n]
            bk = bank()
            mm_group(bk, N, [(ones[:, :], sq3[:, c, 0:N]) for c in range(NCH)], [sqb, onesb])
            ms, msb_ = tmp()
            ts(ms[:, 0:N], ps[:, bk, 0:N], 1.0 / D, ALU.mult, [psb[bk]], [msb_], s2=EPS, op1=ALU.add)
            sd, sdb = tmp()
            act(sd[:, 0:N], ms[:, 0:N], AF.Ln, [msb_], [sdb])
            rs, rsb = tmp()
            act(rs[:, 0:N], sd[:, 0:N], AF.Exp, [sdb], [rsb], scale=-0.5)
            return rs, rsb

        def rmsnorm_post(l, gcol, n):
            off, N = TILES[n]
            rs, rsb = rstd_post(n)
            for c in range(NCH):
                stt(h3[:, c, off:off + N], x3[:, c, off:off + N], vcol(l, gcol + c), rs[:, 0:N],
                    ALU.mult, ALU.mult, [xb[c][n], rsb, vecb], [hb[c][n]])

        def rmsnorm(l, gcol):
            for n in range(len(TILES)):
                rstd_pre(n)
                rmsnorm_post(l, gcol, n)

        def hreads(n):
            return [hb[c][n] for c in range(NCH)]

        def layer(l, ps_, pre_normed, after_ple):
            Win = wview(w_in, l)
            if last and l == layers[-1]:
                T_ = [(32, TILES[0][1] - 32)] + TILES[1:]
            else:
                T_ = TILES
            if last and l == layers[-1]:
                T2_ = [(HALO, TILES[0][0] + TILES[0][1] - HALO)] + TILES[1:]
            else:
                T2_ = T_
            pi = st["p"]
            st["p"] = 1 - pi
            p3, pb = p3s[pi], pbs[pi]
            S.op("pool", lambda e: e.dma_start(out=p3[:, :, :], in_=pT[l, ps_]), writes=[pb], sem="D_p%d" % pi, inc=16)

            if not pre_normed:
                rmsnorm(l, V_GMIX)
            S.op("dve", lambda e: e.memset(a3[:, :, 0:30], 0.0), writes=[apad, RCd, RDd])
            for jp in range(2):
                Vs, Vb = load_slab(Win[:, :, 256 * jp:256 * jp + 256], NCH, 256)
                Gs, Gb = load_slab(Win[:, :, 512 + 256 * jp:512 + 256 * jp + 256], NCH, 256)
                for n, (off, N) in enumerate(T_):
                    for jj in range(2):
                        j = 2 * jp + jj
                        b1, b2 = bank(), bank()
                        mm_group(b1, N, [(Vs[:, k, jj * 128:(jj + 1) * 128], h3[:, k, off:off + N]) for k in range(NCH)],
                                 hreads(n) + [Vb])
                        mm_group(b2, N, [(Gs[:, k, jj * 128:(jj + 1) * 128], h3[:, k, off:off + N]) for k in range(NCH)],
                                 hreads(n) + [Gb])
                        sg, sgb = tmp()
                        act(sg[:, 0:N], ps[:, b2, 0:N], AF.Sigmoid, [psb[b2]], [sgb])
                        dst = a3[:, j, 30 + off:30 + off + N]
                        tt(dst, ps[:, b1, 0:N], sg[:, 0:N], ALU.mult, [psb[b1], sgb], [ab[j][n], RCd, RDd])
                        if n == 0:
                            hv = a3[:, j, 30:30 + HALO]
                            ts(hv, hv, msk[:, ps_:ps_ + 1], ALU.mult, [ab[j][n], mskb], [ab[j][n], RCd, RDd])
            def dg_build(j):
                di = j % 2
                for k in range(31):
                    ts(dg[di][:, k, :], ident[:, :], vcol(l, V_CAW + j * 31 + k), ALU.mult,
                       [identb, vecb], [dgb[di], RCd, RDd])

            def conv_unit(j, n):
                    di = j % 2
                    off, N = T_[n]
                    bk = bank()
                    rd = [dgb[di], apad] + [ab[j][q] for q in range(max(0, n - 1), n + 1)]
                    mm_group(bk, N, [(dg[di][:, k, :], a3[:, j, off + k:off + k + N]) for k in range(31)], rd + [RA])
                    act(csb3[:, j, off:off + N], ps[:, bk, 0:N], AF.Identity, [psb[bk], vecb], [csbb[j][n], RCa, RDa],
                        bias=vcol(l, V_CAB + j))
            lnst = {}

            def ln_pre(n):
                off, N = T_[n]
                crd = [csbb[j][n] for j in range(4)]
                S.op("act", lambda e, off=off, N=N: e.activation(out=sq3[:, 0:4, 0:N], in_=csb3[:, :, off:off + N],
                                                                 func=AF.Identity), reads=crd + [RA], writes=[sqb])
                S.op("act", lambda e, off=off, N=N: e.activation(out=sq3[:, 4:8, 0:N], in_=csb3[:, :, off:off + N],
                                                                 func=AF.Square), reads=crd + [RA], writes=[sqb])

            def ln_stats(n):
                off, N = T_[n]
                b1, b2 = bank(), bank()
                mm_group(b1, N, [(ones[:, :], sq3[:, c, 0:N]) for c in range(4)], [sqb, onesb])
                mm_group(b2, N, [(ones[:, :], sq3[:, 4 + c, 0:N]) for c in range(4)], [sqb, onesb])
                mu, mub = ltmp()
                ts(mu[:, 0:N], ps[:, b1, 0:N], 1.0 / 512, ALU.mult, [psb[b1]], [mub])
                m2, m2b = tmp()
                tt(m2[:, 0:N], mu[:, 0:N], mu[:, 0:N], ALU.mult, [mub], [m2b])
                stt(m2[:, 0:N], ps[:, b2, 0:N], 1.0 / 512, m2[:, 0:N], ALU.mult, ALU.subtract, [psb[b2], m2b], [m2b])
                act(m2[:, 0:N], m2[:, 0:N], AF.Ln, [m2b, vecb], [m2b], bias=vcol(l, V_EPS))
                rs, rsb = ltmp()
                act(rs[:, 0:N], m2[:, 0:N], AF.Exp, [m2b], [rsb], scale=-0.5)
                lnst[n] = (mu, mub, rs, rsb)

            def ln_apply(n):
                off, N = T_[n]
                mu, mub, rs, rsb = lnst[n]
                for j in range(4):
                    xh, xhb = tmp()
                    tt(xh[:, 0:N], csb3[:, j, off:off + N], mu[:, 0:N], ALU.subtract, [csbb[j][n], mub, RA], [xhb])
                    tt(xh[:, 0:N], xh[:, 0:N], rs[:, 0:N], ALU.mult, [xhb, rsb], [xhb])
                    act(a2_3[:, j, off:off + N], xh[:, 0:N], AF.Silu, [xhb, vecb], [a2b[j][n]],
                        bias=vcol(l, V_LNB + j), scale=vcol(l, V_LNG + j))
            zb_slabs = {}

            def zb_load(jp):
                zb_slabs[jp] = (load_slab(Win[:, :, 1024 + 256 * jp:1024 + 256 * jp + 256], NCH, 256),
                                load_slab(Win[:, :, 1536 + 256 * jp:1536 + 256 * jp + 256], NCH, 256),
                                load_slab(Win[:, :, 2048 + 256 * jp:2048 + 256 * jp + 256], NCH, 256))

            def zb_tile(jp, n):
                    (Bs, Bb), (Cs, Cb), (Vs, Vb) = zb_slabs[jp]
                    off, N = T_[n]
                    for jj in range(2):
                        j = 2 * jp + jj
                        si = j % 2
                        cv = stg[si]
                        bb, bc, bv = bank(), bank(), bank()
                        sl = slice(jj * 128, (jj + 1) * 128)
                        mm_group(bc, N, [(Cs[:, k, sl], h3[:, k, off:off + N]) for k in range(NCH)], hreads(n) + [Cb])
                        mm_group(bv, N, [(Vs[:, k, sl], h3[:, k, off:off + N]) for k in range(NCH)], hreads(n) + [Vb])
                        mm_group(bb, N, [(Bs[:, k, sl], h3[:, k, off:off + N]) for k in range(NCH)], hreads(n) + [Bb])
                        vt, vtb = tmp()
                        act(vt[:, 0:N], ps[:, bv, 0:N], AF.Identity, [psb[bv]], [vtb])
                        dst = cv[:, 2 + off:2 + off + N]
                        tt(dst, ps[:, bc, 0:N], vt[:, 0:N], ALU.mult, [psb[bc], vtb], [stgb[si][n]])
                        if n == 0:
                            hv = cv[:, 2:2 + HALO]
                            ts(hv, hv, msk[:, ps_:ps_ + 1], ALU.mult, [stgb[si][n], mskb], [stgb[si][n]])
                        acc, accb = tmp()
                        crd = [stgb[si][q] for q in range(max(0, n - 1), n + 1)] + [stgb[si][0], vecb]
                        act(acc[:, 0:N], cv[:, 2 + off:2 + off + N], AF.Identity, crd, [accb],
                            scale=vcol(l, V_CBW + j * 3 + 2))
                        stt(acc[:, 0:N], cv[:, 1 + off:1 + off + N], vcol(l, V_CBW + j * 3 + 1), acc[:, 0:N],
                            ALU.mult, ALU.add, crd + [accb], [accb])
                        stt(acc[:, 0:N], cv[:, off:off + N], vcol(l, V_CBW + j * 3 + 0), acc[:, 0:N],
                            ALU.mult, ALU.add, crd + [accb], [accb])
                        tt(s3[:, j, off:off + N], ps[:, bb, 0:N], acc[:, 0:N], ALU.mult, [psb[bb], accb], [sbb[j][n]])

            zb_load(0)
            zb_load(1)
            order = [("dg", 0), ("zb", 0, 0), ("cv", 0, 0), ("dg", 1), ("zb", 0, 1), ("cv", 0, 1), ("cv", 0, 2),
                     ("cv", 1, 0), ("dg", 2), ("zb", 0, 2), ("cv", 1, 1), ("cv", 1, 2), ("dg", 3), ("zb", 1, 0),
                     ("cv", 2, 0), ("zb", 1, 1), ("cv", 2, 1), ("cv", 3, 0), ("lnpre", 0), ("zb", 1, 2), ("cv", 3, 1),
                     ("lnstats", 0), ("lnpre", 1), ("cv", 2, 2), ("lnapply", 0), ("lnstats", 1), ("cv", 3, 2),
                     ("lnpre", 2), ("lnapply", 1), ("lnstats", 2), ("lnapply", 2)]
            fns = {"zb": zb_tile, "dg": dg_build, "cv": conv_unit, "lnpre": ln_pre, "lnstats": ln_stats,
                   "lnapply": ln_apply}
            for it in order:
                fns[it[0]](*it[1:])
            WA = wview(w_a_out, l)
            WB = wview(w_b_out, l)
            for mp in range(4):
                c0 = 256 * mp
                As, Ab_ = load_slab(WA[:, :, c0:c0 + 256], 4, 256)
                Bs, Bb = load_slab(WB[:, :, c0:c0 + 256], 4, 256)
                GAs, GAb = load_slab(Win[:, :, 2560 + c0:2560 + c0 + 256], NCH, 256)
                GBs, GBb = load_slab(Win[:, :, 3584 + c0:3584 + c0 + 256], NCH, 256)
                for n, (off, N) in enumerate(T_):
                    for mm in range(2):
                        m = 2 * mp + mm
                        sl = slice(mm * 128, (mm + 1) * 128)
                        bya, bga, byb, bgb = bank(), bank(), bank(), bank()
                        mm_group(bga, N, [(GAs[:, k, sl], h3[:, k, off:off + N]) for k in range(NCH)], hreads(n) + [GAb])
                        mm_group(bgb, N, [(GBs[:, k, sl], h3[:, k, off:off + N]) for k in range(NCH)], hreads(n) + [GBb])
                        mm_group(bya, N, [(As[:, k, sl], a2_3[:, k, off:off + N]) for k in range(4)],
                                 [a2b[k][n] for k in range(4)] + [Ab_])
                        mm_group(byb, N, [(Bs[:, k, sl], s3[:, k, off:off + N]) for k in range(4)],
                                 [sbb[k][n] for k in range(4)] + [Bb])
                        ga, gab = tmp()
                        act(ga[:, 0:N], ps[:, bga, 0:N], AF.Sigmoid, [psb[bga], vecb], [gab], bias=vcol(l, V_BGA + m))
                        gb, gbb = tmp()
                        act(gb[:, 0:N], ps[:, bgb, 0:N], AF.Sigmoid, [psb[bgb], vecb], [gbb], bias=vcol(l, V_BGB + m))
                        tt(ga[:, 0:N], ps[:, bya, 0:N], ga[:, 0:N], ALU.mult, [psb[bya], gab], [gab])
                        tt(gb[:, 0:N], ps[:, byb, 0:N], gb[:, 0:N], ALU.mult, [psb[byb], gbb], [gbb])
                        tt(mg3[:, m, off:off + N], ga[:, 0:N], gb[:, 0:N], ALU.add, [gab, gbb], [mgb[m][n], RA])
            WO = wview(w_o, l)
            wo_slabs = [load_slab(WO[:, :, 256 * mp:256 * mp + 256], NCH, 256) for mp in range(4)]
            for n, (off, N) in enumerate(T_):
                for m in range(NCH):
                    if m == 4 and n > 0:
                        rmsnorm_post(l, V_GFFN, n - 1)
                    if True:
                        Os, Ob = wo_slabs[m // 2]
                        sl = slice((m % 2) * 128, (m % 2 + 1) * 128)
                        bk = bank()
                        mm_group(bk, N, [(Os[:, k, sl], mg3[:, k, off:off + N]) for k in range(NCH)],
                                 [mgb[k][n] for k in range(NCH)] + [Ob, RB])
                        tt(x3[:, m, off:off + N], ps[:, bk, 0:N], x3[:, m, off:off + N], ALU.add,
                           [psb[bk], xb[m][n]], [xb[m][n]])
                rstd_pre(n)
            rmsnorm_post(l, V_GFFN, NT - 1)

            WU = wview(w_up, l)
            pend = []

            def ffn_tail(j, n, acc, accb, bv):
                off, N = T_[n]
                gl, glb = tmp()
                act(gl[:, 0:N], acc[:, 0:N], AF.Gelu_apprx_tanh, [accb], [glb])
                tt(f3[:, j, off:off + N], gl[:, 0:N], ps[:, bv, 0:N], ALU.mult, [glb, psb[bv]], [fb[j][n], RB])

            for jp in range(11):
                Gs, Gb = load_slab(WU[:, :, 256 * jp:256 * jp + 256], NCH, 256)
                Vs, Vb = load_slab(WU[:, :, DFF + 256 * jp:DFF + 256 * jp + 256], NCH, 256)
                for n, (off, N) in enumerate(T_):
                    for jj in range(2):
                        j = 2 * jp + jj
                        si = j % 2
                        ub = stg[si]
                        sl = slice(jj * 128, (jj + 1) * 128)
                        bg, bv = bank(), bank()
                        mm_group(bg, N, [(Gs[:, k, sl], h3[:, k, off:off + N]) for k in range(NCH)], hreads(n) + [Gb])
                        mm_group(bv, N, [(Vs[:, k, sl], h3[:, k, off:off + N]) for k in range(NCH)], hreads(n) + [Vb])
                        dst = ub[:, 2 + off:2 + off + N]
                        act(dst, ps[:, bg, 0:N], AF.Identity, [psb[bg]], [stgb[si][n]])
                        acc, accb = tmp()
                        act(acc[:, 0:N], ps[:, bg, 0:N], AF.Identity, [psb[bg], vecb], [accb],
                            bias=vcol(l, V_CFB + j), scale=vcol(l, V_CFW + j * 3 + 2))
                        if n == 0:
                            hv = ub[:, 2:2 + HALO]
                            ts(hv, hv, msk[:, ps_:ps_ + 1], ALU.mult, [stgb[si][n], mskb], [stgb[si][n]])
                        crd = [stgb[si][q] for q in range(max(0, n - 1), n + 1)] + [stgb[si][0], vecb]
                        stt(acc[:, 0:N], ub[:, 1 + off:1 + off + N], vcol(l, V_CFW + j * 3 + 1), acc[:, 0:N],
                            ALU.mult, ALU.add, crd + [accb], [accb])
                        stt(acc[:, 0:N], ub[:, off:off + N], vcol(l, V_CFW + j * 3 + 0), acc[:, 0:N],
                            ALU.mult, ALU.add, crd + [accb], [accb])
                        if pend:
                            ffn_tail(*pend.pop())
                        pend.append((j, n, acc, accb, bv))
            ffn_tail(*pend.pop())
            WD = wview(w_down, l)
            for mp in range(4):
                c0 = 256 * mp
                parts_k = [(0, 8), (8, 8), (16, 6)]
                slabs = [load_slab(WD[:, k0:k0 + kn, c0:c0 + 256], kn, 256) for (k0, kn) in parts_k]
                for n, (off, N) in enumerate(T2_):
                    for mm in range(2):
                        m = 2 * mp + mm
                        sl = slice(mm * 128, (mm + 1) * 128)
                        bk = bank()
                        parts = []
                        for (k0, kn), (Ds, Db) in zip(parts_k, slabs):
                            parts += [(Ds[:, k, sl], f3[:, k0 + k, off:off + N]) for k in range(kn)]
                        mm_group(bk, N, parts, [fb[k][n] for k in range(NFF)] + [s_[1] for s_ in slabs] + [RCd, RCa])
                        tt(x3[:, m, off:off + N], ps[:, bk, 0:N], x3[:, m, off:off + N], ALU.add,
                           [psb[bk], xb[m][n]], [xb[m][n]])
                    if mp == 3:
                        if n > 0:
                            rmsnorm_post(l, V_GPLE, n - 1)
                        rstd_pre(n)
            rmsnorm_post(l, V_GPLE, NT - 1)

            WG = wview(w_pg, l)
            WP = wview(w_ple, l)
            pg_slabs = [load_slab(WG[:, :, 256 * mp:256 * mp + 256], NCH, 256) for mp in range(4)]
            wp_slabs = [load_slab(WP[:, :, 512 * q:512 * q + 512], 2, 512) for q in range(2)]
            for n, (off, N) in enumerate(T2_):
                for m in range(NCH):
                    if m == 4 and n > 0 and after_ple is not None:
                        after_ple(n - 1)
                    if True:
                        Gs, Gb = pg_slabs[m // 2]
                        sl = slice((m % 2) * 128, (m % 2 + 1) * 128)
                        Ps0, Pb_ = wp_slabs[m // 4]
                        Ps = Ps0[:, :, (m % 4) * 128 - (m % 2) * 128:]
                        bg, bp = bank(), bank()
                        mm_group(bg, N, [(Gs[:, k, sl], h3[:, k, off:off + N]) for k in range(NCH)], hreads(n) + [Gb])
                        mm_group(bp, N, [(Ps[:, k, sl], p3[:, k, off:off + N]) for k in range(2)], [pb, Pb_])
                        pg, pgb = tmp()
                        act(pg[:, 0:N], ps[:, bg, 0:N], AF.Sigmoid, [psb[bg]], [pgb])
                        tt(pg[:, 0:N], ps[:, bp, 0:N], pg[:, 0:N], ALU.mult, [psb[bp], pgb], [pgb])
                        tt(x3[:, m, off:off + N], pg[:, 0:N], x3[:, m, off:off + N], ALU.add,
                           [pgb, xb[m][n]], [xb[m][n]])
                if after_ple is not None:
                    rstd_pre(n)
            if after_ple is not None:
                after_ple(NT - 1)

        allx = [xb[c][n] for c in range(NCH) for n in range(NT)]
        def xload(ps_, n):
            off, N = TILES[n]
            S.op("sp", lambda e: e.dma_start(out=x3[:, :, off:off + N], in_=xT[ps_, :, :, off:off + N]),
                 writes=[xb[c][n] for c in range(NCH)], sem="D_x%d" % n, inc=16)

        def final_tile(ps_, n):
            off, N = TILES[n]
            rs, rsb = rstd_post(n)
            oi = n
            lo = max(off, HALO)
            No = off + N - lo
            for c in range(NCH):
                stt(ost[oi][:, c, 0:No], x3[:, c, lo:lo + No], vcol(0, V_GFIN + c), rs[:, lo - off:lo - off + No],
                    ALU.mult, ALU.mult, [xb[c][n], rsb, vecb], [ostb[oi], RCd, RCa])
            S.op("sp", lambda e: e.dma_start(out=yT[ps_, :, :, lo - HALO:lo - HALO + No], in_=ost[oi][:, :, 0:No]),
                 reads=[ostb[oi], RDd, RDa], sem="D_out%d" % oi, inc=16)
            if ps_ == 0:
                xload(1, n)

        for ps_ in range(2):
            if ps_ == 0 or not last:
                for n in range(NT):
                    xload(ps_, n)
            for li, l in enumerate(layers):
                if li + 1 < len(layers):
                    cb = (lambda n, nxt=layers[li + 1]: rmsnorm_post(nxt, V_GMIX, n))
                elif last and ps_ == 0:
                    def cb(n, ps_=ps_):
                        final_tile(ps_, n)
                        if n >= 1:
                            rstd_pre(n - 1)
                            rmsnorm_post(layers[0], V_GMIX, n - 1)
                elif last:
                    cb = (lambda n, ps_=ps_: final_tile(ps_, n))
                else:
                    cb = None
                pre = li > 0
                if last and ps_ == 1 and li == 0:
                    rstd_pre(NT - 1)
                    rmsnorm_post(layers[0], V_GMIX, NT - 1)
                    pre = True
                layer(l, ps_, pre, cb)
            if not last:
                S.op("sp", lambda e, ps_=ps_: e.dma_start(out=yT[ps_], in_=x3[:, :, :]), reads=allx, sem="D_out0", inc=16)
        out_totals = [(k, v) for k, v in S.count.items() if k.startswith("D_out")]

        sem_names = sorted(S.count.keys())
        sems = {k: es.enter_context(nc.semaphore(k)) for k in sem_names}
        block = es.enter_context(nc.Block())

        def replay(eng_name, e, tail=None):
            for waits, fn, key, inc in S.ops[eng_name]:
                for k, v in waits:
                    e.wait_ge(sems[k], v)
                ins = fn(e)
                ins.then_inc(sems[key], inc)
            if tail is not None:
                tail(e)

        @block.tensor
        def _(e):
            replay("pe", e)

        @block.scalar
        def _(e):
            replay("act", e)

        @block.vector
        def _(e):
            replay("dve", e)

        @block.gpsimd
        def _(e):
            replay("pool", e)

        @block.sync
        def _(e):
            replay("sp", e, tail=lambda e: [e.wait_ge(sems[k], v) for k, v in out_totals])

    return nc


def _pack_vecs(inp):
    def cols(v):
        v = np.asarray(v, np.float32)
        return v.reshape(-1, 128).T

    out = np.zeros((DEPTH, 128, NV), np.float32)
    for l in range(DEPTH):
        o = out[l]
        o[:, V_GMIX:V_GMIX + 8] = cols(inp["g_mix"][l])
        o[:, V_BGA:V_BGA + 8] = cols(inp["b_gate"][l][:D])
        o[:, V_BGB:V_BGB + 8] = cols(inp["b_gate"][l][D:])
        caw = np.asarray(inp["conv_a_w"][l], np.float32)
        for j in range(4):
            o[:, V_CAW + j * 31:V_CAW + (j + 1) * 31] = caw[:, j * 128:(j + 1) * 128].T
        o[:, V_CAB:V_CAB + 4] = cols(inp["conv_a_b"][l])
        o[:, V_LNG:V_LNG + 4] = cols(inp["ln_a_g"][l])
        o[:, V_LNB:V_LNB + 4] = cols(inp["ln_a_b"][l])
        cbw = np.asarray(inp["conv_b_w"][l], np.float32)
        for j in range(4):
            o[:, V_CBW + j * 3:V_CBW + (j + 1) * 3] = cbw[:, j * 128:(j + 1) * 128].T
        o[:, V_GFFN:V_GFFN + 8] = cols(inp["g_ffn"][l])
        cfw = np.asarray(inp["conv_f_w"][l], np.float32)
        for j in range(NFF):
            o[:, V_CFW + j * 3:V_CFW + (j + 1) * 3] = cfw[:, j * 128:(j + 1) * 128].T
        o[:, V_CFB:V_CFB + NFF] = cols(inp["conv_f_b"][l])
        o[:, V_GPLE:V_GPLE + 8] = cols(inp["g_ple"][l])
        o[:, V_GFIN:V_GFIN + 8] = cols(inp["g_final"])
        o[:, V_EPS] = EPS
    return out


def _vc(core, ps_):
    v = 2 * core + ps_
    return v // 8, (v % 8) * OWN


_PROGS = {}


def _prog(layers, first, last):
    key = (tuple(layers), first, last)
    if key not in _PROGS:
        _PROGS[key] = build_program(list(layers), first, last)
    return _PROGS[key]


FUSED = True


def make_in_maps(inputs):
    inp = {k: np.asarray(v) for k, v in inputs.items()}
    x = inp["x"].astype(np.float32, copy=False)
    p = inp["p"].astype(np.float32, copy=False)
    vecs = _pack_vecs(inp)
    ident = np.eye(128, dtype=np.float32)
    wnames = ["w_in", "w_a_out", "w_b_out", "w_o", "w_up", "w_down", "w_ple", "w_ple_gate"]
    shared = {k: np.ascontiguousarray(inp[k], dtype=np.float32) for k in wnames}
    shared["vecs"] = vecs
    shared["ident"] = ident

    in_maps = []
    for c in range(NCORES):
        xT = np.zeros((2, 128, NCH, TT), np.float32)
        pTa = np.zeros((DEPTH, 2, 128, 2, TT), np.float32)
        mask = np.ones((128, 2), np.float32)
        for ps_ in range(2):
            b, t0 = _vc(c, ps_)
            lo = t0 - HALO
            if lo < 0:
                mask[:, ps_] = 0.0
                xs = x[b, 0:t0 + OWN]
                xT[ps_, :, :, HALO:] = xs.T.reshape(NCH, 128, OWN).transpose(1, 0, 2)
                for l in range(DEPTH):
                    pTa[l, ps_, :, :, HALO:] = p[l, b, 0:t0 + OWN].T.reshape(2, 128, OWN).transpose(1, 0, 2)
            else:
                xs = x[b, lo:t0 + OWN]
                xT[ps_] = xs.T.reshape(NCH, 128, TT).transpose(1, 0, 2)
                for l in range(DEPTH):
                    pTa[l, ps_] = p[l, b, lo:t0 + OWN].T.reshape(2, 128, TT).transpose(1, 0, 2)
        m = dict(shared)
        m["xT"] = xT
        m["pT"] = pTa
        m["mask"] = mask
        in_maps.append(m)
    return in_maps


def kernel(**inputs):
    in_maps = make_in_maps(inputs)
    if FUSED:
        nc = _prog(range(DEPTH), True, True)
        res = run_bass_kernel_spmd(nc, in_maps, core_ids=list(range(NCORES)))
        outs = [r["yT"] for r in res.results]
    else:
        outs = None
        for l in range(DEPTH):
            nc = _prog([l], l == 0, l == DEPTH - 1)
            res = run_bass_kernel_spmd(nc, in_maps, core_ids=list(range(NCORES)))
            outs = [r["yT"] for r in res.results]
            if l < DEPTH - 1:
                for c in range(NCORES):
                    in_maps[c]["xT"] = np.ascontiguousarray(outs[c])

    y = np.zeros((2, 8192, D), np.float32)
    for c in range(NCORES):
        for ps_ in range(2):
            b, t0 = _vc(c, ps_)
            yt = outs[c][ps_]
            y[b, t0:t0 + OWN, :] = yt.transpose(2, 1, 0).reshape(OWN, D)
    return y
```

```python
import numpy as np
import concourse.bass as bass
import concourse.mybir as mybir
from concourse.bass_utils import run_bass_kernel_spmd

F32 = mybir.dt.float32
BF16 = mybir.dt.bfloat16
AF = mybir.ActivationFunctionType
ALU = mybir.AluOpType

D = 1024
NCH = 8
HALO = 64
OWN = 1024
TT = HALO + OWN
TILES = [(0, 384), (384, 352), (736, 352)]
DEPTH = 2
DFF = 2816
NFF = 22
EPS = 1e-6
NCORES = 8

V_GMIX = 0
V_BGA = 8
V_BGB = 16
V_CAW = 24
V_CAB = 148
V_LNG = 152
V_LNB = 156
V_CBW = 160
V_GFFN = 172
V_CFW = 180
V_CFB = 246
V_GPLE = 268
V_GFIN = 276
V_EPS = 284
NV = 285

RING_SLOTS = 10
STRICT_SAME_ENGINE = True
SLOT_ELEMS = 2048


class Buf:
    __slots__ = ("name", "w", "r", "coarse")

    def __init__(self, name, coarse=False):
        self.name = name
        self.w = None
        self.r = {}
        self.coarse = coarse


class Sched:
    ENGS = ("pe", "act", "dve", "pool", "sp")

    def __init__(self):
        self.ops = {e: [] for e in self.ENGS}
        self.count = {}
        self.clk = {e: {} for e in self.ENGS}
        self.done = {}

    def op(self, eng, fn, reads=(), writes=(), sem=None, inc=1):
        key = sem if sem is not None else "E_" + eng
        clk = self.clk[eng]
        cand = {}

        def need(k, v):
            if v > clk.get(k, 0) and v > cand.get(k, 0):
                cand[k] = v

        for b in reads:
            if b.w is not None:
                need(*b.w)
        strict = STRICT_SAME_ENGINE and eng in ("act", "dve")
        for b in writes:
            st_b = strict and not b.coarse
            if b.w is not None and (st_b or b.w[0] != key):
                need(*b.w)
            for k, v in b.r.items():
                if st_b or k != key:
                    need(k, v)
        waits = []
        for k in sorted(cand, key=lambda k: (k == key, k)):
            v = cand[k]
            if clk.get(k, 0) >= v:
                continue
            waits.append((k, v))
            for kk, vv in self.done[(k, v)].items():
                if vv > clk.get(kk, 0):
                    clk[kk] = vv
        val = self.count.get(key, 0) + inc
        self.count[key] = val
        d = dict(clk)
        d[key] = val
        self.done[(key, val)] = d
        for b in reads:
            b.r[key] = val
        for b in writes:
            b.w = (key, val)
            b.r = {}
        self.ops[eng].append((waits, fn, key, inc))


def build_program(layers, first, last):
    nc = bass.Bass("TRN2", target_bir_lowering=False)
    S = Sched()

    def dram(name, shape, kind="ExternalInput"):
        return nc.dram_tensor(name, list(shape), F32, kind=kind).ap()

    xT = dram("xT", [2, 128, NCH, TT])
    pT = dram("pT", [DEPTH, 2, 128, 2, TT])
    maskd = dram("mask", [128, 2])
    vecd = dram("vecs", [DEPTH, 128, NV])
    identd = dram("ident", [128, 128])
    w_in = dram("w_in", [DEPTH, D, 4608])
    w_a_out = dram("w_a_out", [DEPTH, 512, D])
    w_b_out = dram("w_b_out", [DEPTH, 512, D])
    w_o = dram("w_o", [DEPTH, D, D])
    w_up = dram("w_up", [DEPTH, D, 2 * DFF])
    w_down = dram("w_down", [DEPTH, DFF, D])
    w_ple = dram("w_ple", [DEPTH, 256, D])
    w_pg = dram("w_ple_gate", [DEPTH, D, D])
    if last:
        yT = dram("yT", [2, 128, NCH, OWN], kind="ExternalOutput")
    else:
        yT = dram("yT", [2, 128, NCH, TT], kind="ExternalOutput")

    def wview(w, l):
        return w[l].rearrange("(kc p) n -> p kc n", p=128)

    from contextlib import ExitStack

    with ExitStack() as es:
        def sb(name, shape, dt):
            return es.enter_context(nc.sbuf_tensor(name, list(shape), dt))

        x3 = sb("x3", [128, NCH, TT], F32)
        h3 = sb("h3", [128, NCH, TT], BF16)
        R = sb("R", [128, 11968], F32)
        a2_3 = sb("a2_3", [128, 4, TT], BF16)
        s3 = sb("s3", [128, 4, TT], BF16)
        sq3 = sb("sq3", [128, NCH, 512], BF16)
        stg = [sb("stg%d" % i, [128, 2 + TT], F32) for i in range(2)]
        NTMP = 6
        NLT = 4
        ltmps = [sb("ltmp%d" % i, [128, 512], F32) for i in range(NLT)]
        tmps = [sb("tmp%d" % i, [128, 512], F32) for i in range(NTMP)]
        ring = [sb("ring%d" % i, [128, SLOT_ELEMS], BF16) for i in range(RING_SLOTS)]
        p3s = [sb("p3_%d" % i, [128, 2, TT], BF16) for i in range(2)]
        vec = sb("vec", [128, DEPTH, NV], F32)
        msk = sb("msk", [128, 2], F32)
        ones = sb("ones", [128, 128], BF16)
        ident = sb("ident_sb", [128, 128], BF16)
        ps = es.enter_context(nc.psum_tensor("ps", [128, 8, 512], F32))

        Rb = R[:, :].bitcast(BF16)
        A_W = 30 + TT
        a3 = Rb[:, 0:4 * A_W].rearrange("p (c t) -> p c t", c=4)
        o = 4 * A_W
        dg = [Rb[:, o + i * 3968: o + (i + 1) * 3968].rearrange("p (k m) -> p k m", k=31) for i in range(2)]
        o += 2 * 3968
        csb3 = R[:, 6204:6204 + 4 * TT].rearrange("p (c t) -> p c t", c=4)
        assert 6204 + 4 * TT <= 11968
        mg3 = Rb[:, 0:NCH * TT].rearrange("p (c t) -> p c t", c=NCH)
        f3 = Rb[:, 0:NFF * TT].rearrange("p (c t) -> p c t", c=NFF)
        ost = [R[:, i * 3072:(i + 1) * 3072].rearrange("p (c t) -> p c t", c=NCH) for i in range(3)]

        NT = len(TILES)
        xb = [[Buf("x%d_%d" % (c, n)) for n in range(NT)] for c in range(NCH)]
        hb = [[Buf("h%d_%d" % (c, n)) for n in range(NT)] for c in range(NCH)]
        ab = [[Buf("a") for n in range(NT)] for j in range(4)]
        apad = Buf("apad")
        dgb = [Buf("dg0", True), Buf("dg1", True)]
        csbb = [[Buf("c") for n in range(NT)] for j in range(4)]
        a2b = [[Buf("a2") for n in range(NT)] for j in range(4)]
        sbb = [[Buf("s") for n in range(NT)] for j in range(4)]
        mgb = [[Buf("mg") for n in range(NT)] for m in range(NCH)]
        fb = [[Buf("f") for n in range(NT)] for j in range(NFF)]
        ostb = [Buf("ost0", True), Buf("ost1", True), Buf("ost2", True)]
        RA, RB = Buf("RA", True), Buf("RB", True)
        RCd, RCa, RDd, RDa = Buf("RCd", True), Buf("RCa", True), Buf("RDd", True), Buf("RDa", True)
        sqb = Buf("sq", True)
        stgb = [[Buf("stg") for n in range(NT)] for i in range(2)]
        tmpb = [Buf("tmp%d" % i) for i in range(NTMP)]
        ltmpb = [Buf("ltmp%d" % i) for i in range(NLT)]
        ringb = [Buf("ring%d" % i) for i in range(RING_SLOTS)]
        pbs = [Buf("p0"), Buf("p1")]
        vecb = Buf("vec")
        mskb = Buf("msk")
        onesb = Buf("ones")
        identb = Buf("ident")
        psb = [Buf("ps%d" % i) for i in range(8)]

        st = {"tmp": 0, "bank": 0, "ring": 0, "ltmp": 0, "p": 0}

        def ltmp():
            i = st["ltmp"]
            st["ltmp"] = (i + 1) % NLT
            return ltmps[i], ltmpb[i]

        def tmp():
            i = st["tmp"]
            st["tmp"] = (i + 1) % NTMP
            return tmps[i], tmpb[i]

        def bank():
            i = st["bank"]
            st["bank"] = (i + 1) % 8
            return i

        def load_slab(src, kc, cols):
            i = st["ring"]
            st["ring"] = (i + 1) % RING_SLOTS
            dst = ring[i][:, 0:kc * cols].rearrange("p (k c) -> p k c", k=kc)
            S.op("pool", lambda e, dst=dst, src=src: e.dma_start(out=dst, in_=src),
                 writes=[ringb[i]], sem="D_ring%d" % i, inc=16)
            return dst, ringb[i]

        def mm_group(bk, N, parts, reads):
            def fn(e, bk=bk, N=N, parts=parts):
                ins = None
                last_i = len(parts) - 1
                for i, (l, r) in enumerate(parts):
                    ins = e.matmul(ps[:, bk, 0:N], lhsT=l, rhs=r, start=(i == 0), stop=(i == last_i))
                return ins
            S.op("pe", fn, reads=reads, writes=[psb[bk]])

        def act(out, in_, func, reads, writes, bias=None, scale=None):
            kw = {}
            if bias is not None:
                kw["bias"] = bias
            if scale is not None:
                kw["scale"] = scale
            S.op("act", lambda e: e.activation(out=out, in_=in_, func=func, **kw), reads=reads, writes=writes)

        def tt(out, in0, in1, op, reads, writes):
            S.op("dve", lambda e: e.tensor_tensor(out=out, in0=in0, in1=in1, op=op), reads=reads, writes=writes)

        def ts(out, in0, s1, op0, reads, writes, s2=None, op1=None):
            if op1 is None:
                S.op("dve", lambda e: e.tensor_scalar(out=out, in0=in0, scalar1=s1, scalar2=None, op0=op0),
                     reads=reads, writes=writes)
            else:
                S.op("dve", lambda e: e.tensor_scalar(out=out, in0=in0, scalar1=s1, scalar2=s2, op0=op0, op1=op1),
                     reads=reads, writes=writes)

        def stt(out, in0, scalar, in1, op0, op1, reads, writes):
            S.op("dve", lambda e: e.scalar_tensor_tensor(out=out, in0=in0, scalar=scalar, in1=in1, op0=op0, op1=op1),
                 reads=reads, writes=writes)

        S.op("sp", lambda e: e.dma_start(out=vec[:, :, :], in_=vecd.rearrange("l p v -> p l v")),
             writes=[vecb], sem="D_vec", inc=16)
        S.op("sp", lambda e: e.dma_start(out=msk[:, :], in_=maskd),
             writes=[mskb], sem="D_msk", inc=16)
        S.op("pool", lambda e: e.dma_start(out=ident[:, :], in_=identd), writes=[identb], sem="D_misc2", inc=16)
        S.op("dve", lambda e: e.memset(ones[:, :], 1.0), writes=[onesb])
        for i in range(2):
            S.op("dve", lambda e, i=i: e.memset(stg[i][:, 0:2], 0.0), writes=[stgb[i][0]])

        def vcol(l, c):
            return vec[:, l, c:c + 1]

        def rstd_pre(n):
            off, N = TILES[n]
            S.op("act", lambda e: e.activation(out=sq3[:, :, 0:N], in_=x3[:, :, off:off + N], func=AF.Square),
                 reads=[xb[c][n] for c in range(NCH)], writes=[sqb])

        def rstd_post(n):
            off, N = TILES[n]
            bk = bank()
            mm_group(bk, N, [(ones[:, :], sq3[:, c, 0:N]) for c in range(NCH)], [sqb, onesb])
            ms, msb_ = tmp()
            ts(ms[:, 0:N], ps[:, bk, 0:N], 1.0 / D, ALU.mult, [psb[bk]], [msb_], s2=EPS, op1=ALU.add)
            sd, sdb = tmp()
            act(sd[:, 0:N], ms[:, 0:N], AF.Ln, [msb_], [sdb])
            rs, rsb = tmp()
            act(rs[:, 0:N], sd[:, 0:N], AF.Exp, [sdb], [rsb], scale=-0.5)
            return rs, rsb

        def rmsnorm_post(l, gcol, n):
            off, N = TILES[n]
            rs, rsb = rstd_post(n)
            for c in range(NCH):
                stt(h3[:, c, off:off + N], x3[:, c, off:off + N], vcol(l, gcol + c), rs[:, 0:N],
                    ALU.mult, ALU.mult, [xb[c][n], rsb, vecb], [hb[c][n]])

        def rmsnorm(l, gcol):
            for n in range(len(TILES)):
                rstd_pre(n)
                rmsnorm_post(l, gcol, n)

        def hreads(n):
            return [hb[c][n] for c in range(NCH)]

        def layer(l, ps_, pre_normed, after_ple):
            Win = wview(w_in, l)
            if last and l == layers[-1]:
                T_ = [(32, TILES[0][1] - 32)] + TILES[1:]
            else:
                T_ = TILES
            if last and l == layers[-1]:
                T2_ = [(HALO, TILES[0][0] + TILES[0][1] - HALO)] + TILES[1:]
            elif last and len(layers) >= 2 and l == layers[-2]:
                T2_ = [(32, TILES[0][1] - 32)] + TILES[1:]
            else:
                T2_ = T_
            pi = st["p"]
            st["p"] = 1 - pi
            p3, pb = p3s[pi], pbs[pi]
            S.op("pool", lambda e: e.dma_start(out=p3[:, :, :], in_=pT[l, ps_]), writes=[pb], sem="D_p%d" % pi, inc=16)

            if not pre_normed:
                rmsnorm(l, V_GMIX)
            S.op("dve", lambda e: e.memset(a3[:, :, 0:30], 0.0), writes=[apad, RCd, RDd])
            for jp in range(2):
                Vs, Vb = load_slab(Win[:, :, 256 * jp:256 * jp + 256], NCH, 256)
                Gs, Gb = load_slab(Win[:, :, 512 + 256 * jp:512 + 256 * jp + 256], NCH, 256)
                for n, (off, N) in enumerate(T_):
                    for jj in range(2):
                        j = 2 * jp + jj
                        b1, b2 = bank(), bank()
                        mm_group(b1, N, [(Vs[:, k, jj * 128:(jj + 1) * 128], h3[:, k, off:off + N]) for k in range(NCH)],
                                 hreads(n) + [Vb])
                        mm_group(b2, N, [(Gs[:, k, jj * 128:(jj + 1) * 128], h3[:, k, off:off + N]) for k in range(NCH)],
                                 hreads(n) + [Gb])
                        sg, sgb = tmp()
                        act(sg[:, 0:N], ps[:, b2, 0:N], AF.Sigmoid, [psb[b2]], [sgb])
                        dst = a3[:, j, 30 + off:30 + off + N]
                        tt(dst, ps[:, b1, 0:N], sg[:, 0:N], ALU.mult, [psb[b1], sgb], [ab[j][n], RCd, RDd])
                        if n == 0:
                            hv = a3[:, j, 30:30 + HALO]
                            ts(hv, hv, msk[:, ps_:ps_ + 1], ALU.mult, [ab[j][n], mskb], [ab[j][n], RCd, RDd])
            def dg_build(j):
                di = j % 2
                for k in range(31):
                    ts(dg[di][:, k, :], ident[:, :], vcol(l, V_CAW + j * 31 + k), ALU.mult,
                       [identb, vecb], [dgb[di], RCd, RDd])

            def conv_unit(j, n):
                    di = j % 2
                    off, N = T_[n]
                    bk = bank()
                    rd = [dgb[di], apad] + [ab[j][q] for q in range(max(0, n - 1), n + 1)]
                    mm_group(bk, N, [(dg[di][:, k, :], a3[:, j, off + k:off + k + N]) for k in range(31)], rd + [RA])
                    act(csb3[:, j, off:off + N], ps[:, bk, 0:N], AF.Identity, [psb[bk], vecb], [csbb[j][n], RCa, RDa],
                        bias=vcol(l, V_CAB + j))
            lnst = {}

            def ln_pre(n):
                off, N = T_[n]
                crd = [csbb[j][n] for j in range(4)]
                S.op("act", lambda e, off=off, N=N: e.activation(out=sq3[:, 0:4, 0:N], in_=csb3[:, :, off:off + N],
                                                                 func=AF.Identity), reads=crd + [RA], writes=[sqb])
                S.op("act", lambda e, off=off, N=N: e.activation(out=sq3[:, 4:8, 0:N], in_=csb3[:, :, off:off + N],
                                                                 func=AF.Square), reads=crd + [RA], writes=[sqb])

            def ln_stats(n):
                off, N = T_[n]
                b1, b2 = bank(), bank()
                mm_group(b1, N, [(ones[:, :], sq3[:, c, 0:N]) for c in range(4)], [sqb, onesb])
                mm_group(b2, N, [(ones[:, :], sq3[:, 4 + c, 0:N]) for c in range(4)], [sqb, onesb])
                mu, mub = ltmp()
                ts(mu[:, 0:N], ps[:, b1, 0:N], 1.0 / 512, ALU.mult, [psb[b1]], [mub])
                m2, m2b = tmp()
                tt(m2[:, 0:N], mu[:, 0:N], mu[:, 0:N], ALU.mult, [mub], [m2b])
                stt(m2[:, 0:N], ps[:, b2, 0:N], 1.0 / 512, m2[:, 0:N], ALU.mult, ALU.subtract, [psb[b2], m2b], [m2b])
                act(m2[:, 0:N], m2[:, 0:N], AF.Ln, [m2b, vecb], [m2b], bias=vcol(l, V_EPS))
                rs, rsb = ltmp()
                act(rs[:, 0:N], m2[:, 0:N], AF.Exp, [m2b], [rsb], scale=-0.5)
                lnst[n] = (mu, mub, rs, rsb)

            def ln_apply(n):
                off, N = T_[n]
                mu, mub, rs, rsb = lnst[n]
                for j in range(4):
                    xh, xhb = tmp()
                    tt(xh[:, 0:N], csb3[:, j, off:off + N], mu[:, 0:N], ALU.subtract, [csbb[j][n], mub, RA], [xhb])
                    tt(xh[:, 0:N], xh[:, 0:N], rs[:, 0:N], ALU.mult, [xhb, rsb], [xhb])
                    act(a2_3[:, j, off:off + N], xh[:, 0:N], AF.Silu, [xhb, vecb], [a2b[j][n]],
                        bias=vcol(l, V_LNB + j), scale=vcol(l, V_LNG + j))
            zb_slabs = {}

            def zb_load(jp):
                zb_slabs[jp] = (load_slab(Win[:, :, 1024 + 256 * jp:1024 + 256 * jp + 256], NCH, 256),
                                load_slab(Win[:, :, 1536 + 256 * jp:1536 + 256 * jp + 256], NCH, 256),
                                load_slab(Win[:, :, 2048 + 256 * jp:2048 + 256 * jp + 256], NCH, 256))

            def zb_tile(jp, n):
                    (Bs, Bb), (Cs, Cb), (Vs, Vb) = zb_slabs[jp]
                    off, N = T_[n]
                    for jj in range(2):
                        j = 2 * jp + jj
                        si = j % 2
                        cv = stg[si]
                        bb, bc, bv = bank(), bank(), bank()
                        sl = slice(jj * 128, (jj + 1) * 128)
                        mm_group(bc, N, [(Cs[:, k, sl], h3[:, k, off:off + N]) for k in range(NCH)], hreads(n) + [Cb])
                        mm_group(bv, N, [(Vs[:, k, sl], h3[:, k, off:off + N]) for k in range(NCH)], hreads(n) + [Vb])
                        mm_group(bb, N, [(Bs[:, k, sl], h3[:, k, off:off + N]) for k in range(NCH)], hreads(n) + [Bb])
                        vt, vtb = tmp()
                        act(vt[:, 0:N], ps[:, bv, 0:N], AF.Identity, [psb[bv]], [vtb])
                        dst = cv[:, 2 + off:2 + off + N]
                        tt(dst, ps[:, bc, 0:N], vt[:, 0:N], ALU.mult, [psb[bc], vtb], [stgb[si][n]])
                        if n == 0:
                            hv = cv[:, 2:2 + HALO]
                            ts(hv, hv, msk[:, ps_:ps_ + 1], ALU.mult, [stgb[si][n], mskb], [stgb[si][n]])
                        acc, accb = tmp()
                        crd = [stgb[si][q] for q in range(max(0, n - 1), n + 1)] + [stgb[si][0], vecb]
                        act(acc[:, 0:N], cv[:, 2 + off:2 + off + N], AF.Identity, crd, [accb],
                            scale=vcol(l, V_CBW + j * 3 + 2))
                        stt(acc[:, 0:N], cv[:, 1 + off:1 + off + N], vcol(l, V_CBW + j * 3 + 1), acc[:, 0:N],
                            ALU.mult, ALU.add, crd + [accb], [accb])
                        stt(acc[:, 0:N], cv[:, off:off + N], vcol(l, V_CBW + j * 3 + 0), acc[:, 0:N],
                            ALU.mult, ALU.add, crd + [accb], [accb])
                        tt(s3[:, j, off:off + N], ps[:, bb, 0:N], acc[:, 0:N], ALU.mult, [psb[bb], accb], [sbb[j][n]])

            zb_load(0)
            zb_load(1)
            order = [("dg", 0), ("zb", 0, 0), ("cv", 0, 0), ("dg", 1), ("zb", 0, 1), ("cv", 0, 1), ("cv", 0, 2),
                     ("cv", 1, 0), ("dg", 2), ("zb", 0, 2), ("cv", 1, 1), ("cv", 1, 2), ("dg", 3), ("zb", 1, 0),
                     ("cv", 2, 0), ("zb", 1, 1), ("cv", 2, 1), ("cv", 3, 0), ("lnpre", 0), ("zb", 1, 2), ("cv", 3, 1),
                     ("lnstats", 0), ("lnpre", 1), ("cv", 2, 2), ("lnapply", 0), ("lnstats", 1), ("cv", 3, 2),
                     ("lnpre", 2), ("lnapply", 1), ("lnstats", 2), ("lnapply", 2)]
            fns = {"zb": zb_tile, "dg": dg_build, "cv": conv_unit, "lnpre": ln_pre, "lnstats": ln_stats,
                   "lnapply": ln_apply}
            for it in order:
                fns[it[0]](*it[1:])
            WA = wview(w_a_out, l)
            WB = wview(w_b_out, l)
            for mp in range(4):
                c0 = 256 * mp
                As, Ab_ = load_slab(WA[:, :, c0:c0 + 256], 4, 256)
                Bs, Bb = load_slab(WB[:, :, c0:c0 + 256], 4, 256)
                GAs, GAb = load_slab(Win[:, :, 2560 + c0:2560 + c0 + 256], NCH, 256)
                GBs, GBb = load_slab(Win[:, :, 3584 + c0:3584 + c0 + 256], NCH, 256)
                for n, (off, N) in enumerate(T_):
                    for mm in range(2):
                        m = 2 * mp + mm
                        sl = slice(mm * 128, (mm + 1) * 128)
                        bya, bga, byb, bgb = bank(), bank(), bank(), bank()
                        mm_group(bga, N, [(GAs[:, k, sl], h3[:, k, off:off + N]) for k in range(NCH)], hreads(n) + [GAb])
                        mm_group(bgb, N, [(GBs[:, k, sl], h3[:, k, off:off + N]) for k in range(NCH)], hreads(n) + [GBb])
                        mm_group(bya, N, [(As[:, k, sl], a2_3[:, k, off:off + N]) for k in range(4)],
                                 [a2b[k][n] for k in range(4)] + [Ab_])
                        mm_group(byb, N, [(Bs[:, k, sl], s3[:, k, off:off + N]) for k in range(4)],
                                 [sbb[k][n] for k in range(4)] + [Bb])
                        ga, gab = tmp()
                        act(ga[:, 0:N], ps[:, bga, 0:N], AF.Sigmoid, [psb[bga], vecb], [gab], bias=vcol(l, V_BGA + m))
                        gb, gbb = tmp()
                        act(gb[:, 0:N], ps[:, bgb, 0:N], AF.Sigmoid, [psb[bgb], vecb], [gbb], bias=vcol(l, V_BGB + m))
                        tt(ga[:, 0:N], ps[:, bya, 0:N], ga[:, 0:N], ALU.mult, [psb[bya], gab], [gab])
                        tt(gb[:, 0:N], ps[:, byb, 0:N], gb[:, 0:N], ALU.mult, [psb[byb], gbb], [gbb])
                        tt(mg3[:, m, off:off + N], ga[:, 0:N], gb[:, 0:N], ALU.add, [gab, gbb], [mgb[m][n], RA])
            WO = wview(w_o, l)
            wo_slabs = [load_slab(WO[:, :, 256 * mp:256 * mp + 256], NCH, 256) for mp in range(4)]
            for n, (off, N) in enumerate(T_):
                for m in range(NCH):
                    if m == 4 and n > 0:
                        rmsnorm_post(l, V_GFFN, n - 1)
                    if True:
                        Os, Ob = wo_slabs[m // 2]
                        sl = slice((m % 2) * 128, (m % 2 + 1) * 128)
                        bk = bank()
                        mm_group(bk, N, [(Os[:, k, sl], mg3[:, k, off:off + N]) for k in range(NCH)],
                                 [mgb[k][n] for k in range(NCH)] + [Ob, RB])
                        tt(x3[:, m, off:off + N], ps[:, bk, 0:N], x3[:, m, off:off + N], ALU.add,
                           [psb[bk], xb[m][n]], [xb[m][n]])
                rstd_pre(n)
            rmsnorm_post(l, V_GFFN, NT - 1)

            WU = wview(w_up, l)
            pend = []

            def ffn_tail(j, n, acc, accb, bv):
                off, N = T_[n]
                gl, glb = tmp()
                act(gl[:, 0:N], acc[:, 0:N], AF.Gelu_apprx_tanh, [accb], [glb])
                tt(f3[:, j, off:off + N], gl[:, 0:N], ps[:, bv, 0:N], ALU.mult, [glb, psb[bv]], [fb[j][n], RB])

            for jp in range(11):
                Gs, Gb = load_slab(WU[:, :, 256 * jp:256 * jp + 256], NCH, 256)
                Vs, Vb = load_slab(WU[:, :, DFF + 256 * jp:DFF + 256 * jp + 256], NCH, 256)
                for n, (off, N) in enumerate(T_):
                    for jj in range(2):
                        j = 2 * jp + jj
                        si = j % 2
                        ub = stg[si]
                        sl = slice(jj * 128, (jj + 1) * 128)
                        bg, bv = bank(), bank()
                        mm_group(bg, N, [(Gs[:, k, sl], h3[:, k, off:off + N]) for k in range(NCH)], hreads(n) + [Gb])
                        mm_group(bv, N, [(Vs[:, k, sl], h3[:, k, off:off + N]) for k in range(NCH)], hreads(n) + [Vb])
                        dst = ub[:, 2 + off:2 + off + N]
                        act(dst, ps[:, bg, 0:N], AF.Identity, [psb[bg]], [stgb[si][n]])
                        acc, accb = tmp()
                        act(acc[:, 0:N], ps[:, bg, 0:N], AF.Identity, [psb[bg], vecb], [accb],
                            bias=vcol(l, V_CFB + j), scale=vcol(l, V_CFW + j * 3 + 2))
                        if n == 0:
                            hv = ub[:, 2:2 + HALO]
                            ts(hv, hv, msk[:, ps_:ps_ + 1], ALU.mult, [stgb[si][n], mskb], [stgb[si][n]])
                        crd = [stgb[si][q] for q in range(max(0, n - 1), n + 1)] + [stgb[si][0], vecb]
                        stt(acc[:, 0:N], ub[:, 1 + off:1 + off + N], vcol(l, V_CFW + j * 3 + 1), acc[:, 0:N],
                            ALU.mult, ALU.add, crd + [accb], [accb])
                        stt(acc[:, 0:N], ub[:, off:off + N], vcol(l, V_CFW + j * 3 + 0), acc[:, 0:N],
                            ALU.mult, ALU.add, crd + [accb], [accb])
                        if pend:
                            ffn_tail(*pend.pop())
                        pend.append((j, n, acc, accb, bv))
            ffn_tail(*pend.pop())
            WD = wview(w_down, l)
            for mp in range(4):
                c0 = 256 * mp
                parts_k = [(0, 8), (8, 8), (16, 6)]
                slabs = [load_slab(WD[:, k0:k0 + kn, c0:c0 + 256], kn, 256) for (k0, kn) in parts_k]
                for n, (off, N) in enumerate(T2_):
                    for mm in range(2):
                        m = 2 * mp + mm
                        sl = slice(mm * 128, (mm + 1) * 128)
                        bk = bank()
                        parts = []
                        for (k0, kn), (Ds, Db) in zip(parts_k, slabs):
                            parts += [(Ds[:, k, sl], f3[:, k0 + k, off:off + N]) for k in range(kn)]
                        mm_group(bk, N, parts, [fb[k][n] for k in range(NFF)] + [s_[1] for s_ in slabs] + [RCd, RCa])
                        tt(x3[:, m, off:off + N], ps[:, bk, 0:N], x3[:, m, off:off + N], ALU.add,
                           [psb[bk], xb[m][n]], [xb[m][n]])
                    if mp == 3:
                        if n > 0:
                            rmsnorm_post(l, V_GPLE, n - 1)
                        rstd_pre(n)
            rmsnorm_post(l, V_GPLE, NT - 1)

            WG = wview(w_pg, l)
            WP = wview(w_ple, l)
            pg_slabs = [load_slab(WG[:, :, 256 * mp:256 * mp + 256], NCH, 256) for mp in range(4)]
            wp_slabs = [load_slab(WP[:, :, 512 * q:512 * q + 512], 2, 512) for q in range(2)]
            for n, (off, N) in enumerate(T2_):
                for m in range(NCH):
                    if m == 4 and n > 0 and after_ple is not None:
                        after_ple(n - 1)
                    if True:
                        Gs, Gb = pg_slabs[m // 2]
                        sl = slice((m % 2) * 128, (m % 2 + 1) * 128)
                        Ps0, Pb_ = wp_slabs[m // 4]
                        Ps = Ps0[:, :, (m % 4) * 128 - (m % 2) * 128:]
                        bg, bp = bank(), bank()
                        mm_group(bg, N, [(Gs[:, k, sl], h3[:, k, off:off + N]) for k in range(NCH)], hreads(n) + [Gb])
                        mm_group(bp, N, [(Ps[:, k, sl], p3[:, k, off:off + N]) for k in range(2)], [pb, Pb_])
                        pg, pgb = tmp()
                        act(pg[:, 0:N], ps[:, bg, 0:N], AF.Sigmoid, [psb[bg]], [pgb])
                        tt(pg[:, 0:N], ps[:, bp, 0:N], pg[:, 0:N], ALU.mult, [psb[bp], pgb], [pgb])
                        tt(x3[:, m, off:off + N], pg[:, 0:N], x3[:, m, off:off + N], ALU.add,
                           [pgb, xb[m][n]], [xb[m][n]])
                if after_ple is not None:
                    rstd_pre(n)
            if after_ple is not None:
                after_ple(NT - 1)

        allx = [xb[c][n] for c in range(NCH) for n in range(NT)]
        def xload(ps_, n):
            off, N = TILES[n]
            S.op("sp", lambda e: e.dma_start(out=x3[:, :, off:off + N], in_=xT[ps_, :, :, off:off + N]),
                 writes=[xb[c][n] for c in range(NCH)], sem="D_x%d" % n, inc=16)

        def final_tile(ps_, n):
            off, N = TILES[n]
            rs, rsb = rstd_post(n)
            oi = n
            lo = max(off, HALO)
            No = off + N - lo
            for c in range(NCH):
                stt(ost[oi][:, c, 0:No], x3[:, c, lo:lo + No], vcol(0, V_GFIN + c), rs[:, lo - off:lo - off + No],
                    ALU.mult, ALU.mult, [xb[c][n], rsb, vecb], [ostb[oi], RCd, RCa])
            S.op("sp", lambda e: e.dma_start(out=yT[ps_, :, :, lo - HALO:lo - HALO + No], in_=ost[oi][:, :, 0:No]),
                 reads=[ostb[oi], RDd, RDa], sem="D_out%d" % oi, inc=16)
            if ps_ == 0:
                xload(1, n)

        for ps_ in range(2):
            if ps_ == 0 or not last:
                for n in range(NT):
                    xload(ps_, n)
            for li, l in enumerate(layers):
                if li + 1 < len(layers):
                    cb = (lambda n, nxt=layers[li + 1]: rmsnorm_post(nxt, V_GMIX, n))
                elif last and ps_ == 0:
                    def cb(n, ps_=ps_):
                        final_tile(ps_, n)
                        if n >= 1:
                            rstd_pre(n - 1)
                            rmsnorm_post(layers[0], V_GMIX, n - 1)
                elif last:
                    cb = (lambda n, ps_=ps_: final_tile(ps_, n))
                else:
                    cb = None
                pre = li > 0
                if last and ps_ == 1 and li == 0:
                    rstd_pre(NT - 1)
                    rmsnorm_post(layers[0], V_GMIX, NT - 1)
                    pre = True
                layer(l, ps_, pre, cb)
            if not last:
                S.op("sp", lambda e, ps_=ps_: e.dma_start(out=yT[ps_], in_=x3[:, :, :]), reads=allx, sem="D_out0", inc=16)
        out_totals = [(k, v) for k, v in S.count.items() if k.startswith("D_out")]

        sem_names = sorted(S.count.keys())
        sems = {k: es.enter_context(nc.semaphore(k)) for k in sem_names}
        block = es.enter_context(nc.Block())

        def replay(eng_name, e, tail=None):
            for waits, fn, key, inc in S.ops[eng_name]:
                for k, v in waits:
                    e.wait_ge(sems[k], v)
                ins = fn(e)
                ins.then_inc(sems[key], inc)
            if tail is not None:
                tail(e)

        @block.tensor
        def _(e):
            replay("pe", e)

        @block.scalar
        def _(e):
            replay("act", e)

        @block.vector
        def _(e):
            replay("dve", e)

        @block.gpsimd
        def _(e):
            replay("pool", e)

        @block.sync
        def _(e):
            replay("sp", e, tail=lambda e: [e.wait_ge(sems[k], v) for k, v in out_totals])

    return nc


def _pack_vecs(inp):
    def cols(v):
        v = np.asarray(v, np.float32)
        return v.reshape(-1, 128).T

    out = np.zeros((DEPTH, 128, NV), np.float32)
    for l in range(DEPTH):
        o = out[l]
        o[:, V_GMIX:V_GMIX + 8] = cols(inp["g_mix"][l])
        o[:, V_BGA:V_BGA + 8] = cols(inp["b_gate"][l][:D])
        o[:, V_BGB:V_BGB + 8] = cols(inp["b_gate"][l][D:])
        caw = np.asarray(inp["conv_a_w"][l], np.float32)
        for j in range(4):
            o[:, V_CAW + j * 31:V_CAW + (j + 1) * 31] = caw[:, j * 128:(j + 1) * 128].T
        o[:, V_CAB:V_CAB + 4] = cols(inp["conv_a_b"][l])
        o[:, V_LNG:V_LNG + 4] = cols(inp["ln_a_g"][l])
        o[:, V_LNB:V_LNB + 4] = cols(inp["ln_a_b"][l])
        cbw = np.asarray(inp["conv_b_w"][l], np.float32)
        for j in range(4):
            o[:, V_CBW + j * 3:V_CBW + (j + 1) * 3] = cbw[:, j * 128:(j + 1) * 128].T
        o[:, V_GFFN:V_GFFN + 8] = cols(inp["g_ffn"][l])
        cfw = np.asarray(inp["conv_f_w"][l], np.float32)
        for j in range(NFF):
            o[:, V_CFW + j * 3:V_CFW + (j + 1) * 3] = cfw[:, j * 128:(j + 1) * 128].T
        o[:, V_CFB:V_CFB + NFF] = cols(inp["conv_f_b"][l])
        o[:, V_GPLE:V_GPLE + 8] = cols(inp["g_ple"][l])
        o[:, V_GFIN:V_GFIN + 8] = cols(inp["g_final"])
        o[:, V_EPS] = EPS
    return out


def _vc(core, ps_):
    v = 2 * core + ps_
    return v // 8, (v % 8) * OWN


_PROGS = {}


def _prog(layers, first, last):
    key = (tuple(layers), first, last)
    if key not in _PROGS:
        _PROGS[key] = build_program(list(layers), first, last)
    return _PROGS[key]


FUSED = True


def make_in_maps(inputs):
    inp = {k: np.asarray(v) for k, v in inputs.items()}
    x = inp["x"].astype(np.float32, copy=False)
    p = inp["p"].astype(np.float32, copy=False)
    vecs = _pack_vecs(inp)
    ident = np.eye(128, dtype=np.float32)
    wnames = ["w_in", "w_a_out", "w_b_out", "w_o", "w_up", "w_down", "w_ple", "w_ple_gate"]
    shared = {k: np.ascontiguousarray(inp[k], dtype=np.float32) for k in wnames}
    shared["vecs"] = vecs
    shared["ident"] = ident

    in_maps = []
    for c in range(NCORES):
        xT = np.zeros((2, 128, NCH, TT), np.float32)
        pTa = np.zeros((DEPTH, 2, 128, 2, TT), np.float32)
        mask = np.ones((128, 2), np.float32)
        for ps_ in range(2):
            b, t0 = _vc(c, ps_)
            lo = t0 - HALO
            if lo < 0:
                mask[:, ps_] = 0.0
                xs = x[b, 0:t0 + OWN]
                xT[ps_, :, :, HALO:] = xs.T.reshape(NCH, 128, OWN).transpose(1, 0, 2)
                for l in range(DEPTH):
                    pTa[l, ps_, :, :, HALO:] = p[l, b, 0:t0 + OWN].T.reshape(2, 128, OWN).transpose(1, 0, 2)
            else:
                xs = x[b, lo:t0 + OWN]
                xT[ps_] = xs.T.reshape(NCH, 128, TT).transpose(1, 0, 2)
                for l in range(DEPTH):
                    pTa[l, ps_] = p[l, b, lo:t0 + OWN].T.reshape(2, 128, TT).transpose(1, 0, 2)
        m = dict(shared)
        m["xT"] = xT
        m["pT"] = pTa
        m["mask"] = mask
        in_maps.append(m)
    return in_maps


def kernel(**inputs):
    in_maps = make_in_maps(inputs)
    if FUSED:
        nc = _prog(range(DEPTH), True, True)
        res = run_bass_kernel_spmd(nc, in_maps, core_ids=list(range(NCORES)))
        outs = [r["yT"] for r in res.results]
    else:
        outs = None
        for l in range(DEPTH):
            nc = _prog([l], l == 0, l == DEPTH - 1)
            res = run_bass_kernel_spmd(nc, in_maps, core_ids=list(range(NCORES)))
            outs = [r["yT"] for r in res.results]
            if l < DEPTH - 1:
                for c in range(NCORES):
                    in_maps[c]["xT"] = np.ascontiguousarray(outs[c])

    y = np.zeros((2, 8192, D), np.float32)
    for c in range(NCORES):
        for ps_ in range(2):
            b, t0 = _vc(c, ps_)
            yt = outs[c][ps_]
            y[b, t0:t0 + OWN, :] = yt.transpose(2, 1, 0).reshape(OWN, D)
    return y
```
